# Optimizing a Trainium2 kernel written in Bass

```python
import jax, jax.numpy as jnp
from jax import lax
import numpy as np

D_MODEL = 2048
BATCH = 2
SEQ = 8192
DEPTH = 2

D_MIX = D_MODEL
ATT_HEAD_DIM = 128
D_ATT = D_MIX // 2
ATT_HEADS = D_ATT // ATT_HEAD_DIM
D_RNN = D_MIX // 4
RNN_HEADS = 4
RNN_HEAD_DIM = D_RNN // RNN_HEADS
POOL_WINDOWS = (2, 4, 8, 16)
D_POOL = D_MIX - D_ATT - D_RNN
POOL_GROUPS = len(POOL_WINDOWS)
POOL_GROUP_DIM = D_POOL // POOL_GROUPS
D_IN_PROJ = 3 * D_ATT + 2 * D_RNN + D_POOL
CONV_WIDTH = 4
RG_C = 8.0
D_FF = 5632
Q_BLOCK = 128
NORM_EPS = 1e-6

kernel_name = "hybrid_stickbreak_rglru_pool_macaron"


def rms_norm(x, g):
    xf = x.astype(jnp.float32)
    y = xf * lax.rsqrt(jnp.mean(xf * xf, axis=-1, keepdims=True) + NORM_EPS)
    return (y * g.astype(jnp.float32)).astype(x.dtype)


def swiglu(h, w_gate, w_up, w_down):
    return (jax.nn.silu(h @ w_gate) * (h @ w_up)) @ w_down


def stick_breaking_attention(q, k, v):
    B, S, H, dh = q.shape
    scale = dh ** -0.5
    outs = []
    for blk in range(S // Q_BLOCK):
        start, end = blk * Q_BLOCK, (blk + 1) * Q_BLOCK
        qb = q[:, start:end]
        kb, vb = k[:, :end], v[:, :end]
        z = jnp.einsum('bthd,bshd->bhts', qb, kb,
                       preferred_element_type=jnp.float32) * scale
        t_pos = start + jnp.arange(Q_BLOCK, dtype=jnp.int32)
        s_pos = jnp.arange(end, dtype=jnp.int32)
        causal = s_pos[None, :] < t_pos[:, None]
        log_fail = jnp.where(causal, jax.nn.log_sigmoid(-z), 0.0)
        after = lax.cumsum(log_fail, axis=3, reverse=True) - log_fail
        w = jnp.where(causal, jnp.exp(jax.nn.log_sigmoid(z) + after), 0.0)
        outs.append(jnp.einsum('bhts,bshd->bthd', w.astype(vb.dtype), vb))
    return jnp.concatenate(outs, axis=1)


def causal_depthwise_conv(x, w, b):
    C = x.shape[-1]
    y = lax.conv_general_dilated(
        x, w[:, None, :], window_strides=(1,), padding=[(CONV_WIDTH - 1, 0)],
        dimension_numbers=('NWC', 'WIO', 'NWC'), feature_group_count=C)
    return y + b


def _scan_combine(c1, c2):
    a1, b1 = c1
    a2, b2 = c2
    return a1 * a2, a2 * b1 + b2


def rg_lru_mixer(xg, xr, conv_w, conv_b, w_a, b_a, w_x, b_x, lam):
    B, S, _ = xr.shape
    u = causal_depthwise_conv(xr, conv_w, conv_b)
    uh = u.reshape(B, S, RNN_HEADS, RNN_HEAD_DIM)
    r = jax.nn.sigmoid(jnp.einsum('bshi,hij->bshj', uh, w_a).reshape(B, S, D_RNN) + b_a)
    i = jax.nn.sigmoid(jnp.einsum('bshi,hij->bshj', uh, w_x).reshape(B, S, D_RNN) + b_x)
    log_a = RG_C * r.astype(jnp.float32) * jax.nn.log_sigmoid(lam.astype(jnp.float32))
    a = jnp.exp(log_a)
    b = jnp.sqrt(-jnp.expm1(2.0 * log_a)) * (i * u).astype(jnp.float32)
    _, h = lax.associative_scan(_scan_combine, (a, b), axis=1)
    return jax.nn.gelu(xg) * h.astype(xg.dtype)


def pool_mixer(xp, w_pool, scale):
    B, S, _ = xp.shape
    xg = xp.reshape(B, S, POOL_GROUPS, POOL_GROUP_DIM).astype(jnp.float32)
    cs = jnp.cumsum(xg, axis=1)
    t = jnp.arange(S, dtype=jnp.int32)
    outs = []
    for g, win in enumerate(POOL_WINDOWS):
        c = cs[:, :, g]
        lower = jnp.pad(c, ((0, 0), (win, 0), (0, 0)))[:, :S]
        count = jnp.minimum(t + 1, win).astype(jnp.float32)[None, :, None]
        outs.append((c - lower) / count - xg[:, :, g])
    d = jnp.stack(outs, axis=2).astype(xp.dtype)
    y = jnp.einsum('bsgi,gij->bsgj', d, w_pool).reshape(B, S, D_POOL)
    return y * scale


def setup_inputs(seed: int = 0) -> dict:
    key = jax.random.key(seed)
    ks = jax.random.split(key, 24)
    f32 = jnp.float32

    def nrm(k, shape, fan_in):
        return jax.random.normal(k, shape, f32) * (fan_in ** -0.5)

    def gain(k, shape):
        return 1.0 + 0.02 * jax.random.normal(k, shape, f32)

    def bias(k, shape):
        return 0.02 * jax.random.normal(k, shape, f32)

    L = DEPTH
    out_scale = (2.0 * DEPTH) ** -0.5
    u = jax.random.uniform(ks[13], (L, D_RNN), f32, 0.9, 0.999)
    s = u ** (1.0 / RG_C)
    rg_lambda = jnp.log(s) - jnp.log1p(-s)
    return {
        "x": jax.random.normal(ks[0], (BATCH, SEQ, D_MODEL), f32),
        "norm_ffn1": gain(ks[1], (L, D_MODEL)),
        "ffn1_gate": nrm(ks[2], (L, D_MODEL, D_FF), D_MODEL),
        "ffn1_up": nrm(ks[3], (L, D_MODEL, D_FF), D_MODEL),
        "ffn1_down": nrm(ks[4], (L, D_FF, D_MODEL), D_FF) * out_scale,
        "norm_mix": gain(ks[5], (L, D_MODEL)),
        "w_in": nrm(ks[6], (L, D_MODEL, D_IN_PROJ), D_MODEL),
        "conv_w": nrm(ks[7], (L, CONV_WIDTH, D_RNN), CONV_WIDTH),
        "conv_b": bias(ks[8], (L, D_RNN)),
        "rg_w_a": nrm(ks[9], (L, RNN_HEADS, RNN_HEAD_DIM, RNN_HEAD_DIM), RNN_HEAD_DIM),
        "rg_b_a": bias(ks[10], (L, D_RNN)),
        "rg_w_x": nrm(ks[11], (L, RNN_HEADS, RNN_HEAD_DIM, RNN_HEAD_DIM), RNN_HEAD_DIM),
        "rg_b_x": bias(ks[12], (L, D_RNN)),
        "rg_lambda": rg_lambda,
        "pool_w": nrm(ks[14], (L, POOL_GROUPS, POOL_GROUP_DIM, POOL_GROUP_DIM), POOL_GROUP_DIM),
        "pool_scale": gain(ks[15], (L, D_POOL)),
        "w_out": nrm(ks[16], (L, D_MIX, D_MODEL), D_MIX) * out_scale,
        "norm_ffn2": gain(ks[17], (L, D_MODEL)),
        "ffn2_gate": nrm(ks[18], (L, D_MODEL, D_FF), D_MODEL),
        "ffn2_up": nrm(ks[19], (L, D_MODEL, D_FF), D_MODEL),
        "ffn2_down": nrm(ks[20], (L, D_FF, D_MODEL), D_FF) * out_scale,
        "norm_final": gain(ks[21], (D_MODEL,)),
    }


def reference(x, norm_ffn1, ffn1_gate, ffn1_up, ffn1_down, norm_mix, w_in, conv_w, conv_b,
              rg_w_a, rg_b_a, rg_w_x, rg_b_x, rg_lambda, pool_w, pool_scale, w_out,
              norm_ffn2, ffn2_gate, ffn2_up, ffn2_down, norm_final):
    B, S, _ = x.shape
    splits = [int(v) for v in np.cumsum([D_ATT, D_ATT, D_ATT, D_RNN, D_RNN])]
    for l in range(DEPTH):
        x = x + 0.5 * swiglu(rms_norm(x, norm_ffn1[l]), ffn1_gate[l], ffn1_up[l], ffn1_down[l])
        h = rms_norm(x, norm_mix[l])
        z = h @ w_in[l]
        q, k, v, rg_gate, rg_x, pool_in = jnp.split(z, splits, axis=-1)
        att = stick_breaking_attention(
            q.reshape(B, S, ATT_HEADS, ATT_HEAD_DIM),
            k.reshape(B, S, ATT_HEADS, ATT_HEAD_DIM),
            v.reshape(B, S, ATT_HEADS, ATT_HEAD_DIM)).reshape(B, S, D_ATT)
        rnn = rg_lru_mixer(rg_gate, rg_x, conv_w[l], conv_b[l], rg_w_a[l], rg_b_a[l],
                           rg_w_x[l], rg_b_x[l], rg_lambda[l])
        pool = pool_mixer(pool_in, pool_w[l], pool_scale[l])
        x = x + jnp.concatenate([att, rnn, pool], axis=-1) @ w_out[l]
        x = x + 0.5 * swiglu(rms_norm(x, norm_ffn2[l]), ffn2_gate[l], ffn2_up[l], ffn2_down[l])
    return rms_norm(x, norm_final)
```

```python
import numpy as np
import ml_dtypes
from contextlib import ExitStack

import concourse.bass as bass
import concourse.mybir as mybir
from concourse.bass_utils import run_bass_kernel_spmd

F32 = mybir.dt.float32
BF16 = mybir.dt.bfloat16
ALU = mybir.AluOpType
AF = mybir.ActivationFunctionType

D = 2048
DFF = 5632
S = 8192
NB = 2
DEPTH = 2
TOK = 2048
DIN = 4608
EPS = 1e-6
NCORE = 8


class Buf:
    __slots__ = ("w", "r")

    def __init__(self):
        self.w = None
        self.r = {}


def bufs(n):
    return [Buf() for _ in range(n)]


class Prog:
    ENG = ("pe", "act", "dve", "pool", "sp")
    NDMA = 24

    def __init__(self, nc, es):
        self.nc = nc
        self.ops = {e: [] for e in self.ENG}
        self.cnt = {"pe": 0, "act": 0, "dve": 0, "pool": 0}
        self.seen = {e: {} for e in self.ENG}
        self.dma_n = {"sp": 0, "pool": 0}
        self.dma_last = {}
        self.sems = {}
        for k in ("pe", "act", "dve", "pool", "cc"):
            self.sems[k] = es.enter_context(nc.semaphore("s_" + k))
        for q in ("sp", "pool"):
            for i in range(self.NDMA):
                self.sems[("d", q, i)] = es.enter_context(nc.semaphore("d_%s%d" % (q, i)))

    def _deps(self, reads, writes):
        d = []
        for b in reads:
            if b.w is not None:
                d.append(b.w)
        for b in writes:
            if b.w is not None:
                d.append(b.w)
            d.extend(b.r.items())
        return d

    def _waits(self, eng, deps):
        need = {}
        for k, v in deps:
            if v > need.get(k, 0):
                need[k] = v
        out = []
        seen = self.seen[eng]
        for k, v in need.items():
            if k == eng and eng == "pe":
                continue
            if seen.get(k, 0) >= v:
                continue
            seen[k] = v
            out.append((k, v))
        return out

    def _mark(self, tok, reads, writes):
        for b in reads:
            if b.r.get(tok[0], 0) < tok[1]:
                b.r[tok[0]] = tok[1]
        for b in writes:
            b.w = tok
            b.r = {}

    def op(self, eng, name, reads=(), writes=(), **kw):
        waits = self._waits(eng, self._deps(reads, writes))
        self.cnt[eng] += 1
        tok = (eng, self.cnt[eng])
        self.ops[eng].append((waits, name, kw, tok, 1))
        self._mark(tok, reads, writes)
        return tok

    def dma(self, q, out, in_, reads=(), writes=(), **kw):
        n = self.dma_n[q]
        self.dma_n[q] += 1
        k = ("d", q, n % self.NDMA)
        val = 16 * (n // self.NDMA + 1)
        deps = self._deps(reads, writes)
        if val > 16:
            deps.append((k, val - 16))
        waits = self._waits(q, deps)
        tok = (k, val)
        kw = dict(kw)
        kw["out"] = out
        kw["in_"] = in_
        self.ops[q].append((waits, "dma_start", kw, tok, 16))
        self.dma_last[k] = val
        self._mark(tok, reads, writes)
        return tok

    def idma(self, out, in_, idx, reads=(), writes=()):
        return self.dma("pool", out, in_, reads=reads, writes=writes, _idx=idx)

    def allgather(self, src, dst, deps):
        waits = self._waits("pool", list(deps))
        self.ncc = getattr(self, "ncc", 0) + 1
        tok = ("cc", self.ncc)
        kw = dict(kind="AllGather", op=ALU.bypass, replica_groups=[[0, 1, 2, 3], [4, 5, 6, 7]],
                  ins=[src.opt()], outs=[dst.opt()])
        self.ops["pool"].append((waits, "collective_compute", kw, tok, None))
        self.dma_last["cc"] = self.ncc
        return tok

    def barrier(self, cc=None):
        deps = [(k, v) for k, v in self.cnt.items() if v > 0]
        deps += [(k, v) for k, v in self.dma_last.items() if not (k == "cc" and cc is not None)]
        if cc is not None and cc > 0:
            deps.append(("cc", cc))
        for e in self.ENG:
            waits = self._waits(e, deps)
            if waits:
                self.ops[e].append((waits, None, None, None, 0))

    def _run(self, e, lst):
        for waits, name, kw, tok, inc in lst:
            for k, v in waits:
                e.wait_ge(self.sems[k], v)
            if name is None:
                continue
            if name == "dma_start" and "_idx" in kw:
                kw = dict(kw)
                idx = kw.pop("_idx")
                ins = e.indirect_dma_start(out=kw["out"], out_offset=None, in_=kw["in_"],
                                           in_offset=bass.IndirectOffsetOnAxis(ap=idx, axis=0))
            else:
                ins = getattr(e, name)(**kw)
            if inc is None:
                ins.then_inc(self.sems[tok[0]])
            else:
                ins.then_inc(self.sems[tok[0]], inc)

    def emit(self):
        ops = self.ops
        self.ops = {e: [] for e in self.ENG}
        with self.nc.Block() as block:
            @block.tensor
            def _(e):
                self._run(e, ops["pe"])

            @block.scalar
            def _(e):
                self._run(e, ops["act"])

            @block.vector
            def _(e):
                self._run(e, ops["dve"])

            @block.gpsimd
            def _(e):
                self._run(e, ops["pool"])

            @block.sync
            def _(e):
                self._run(e, ops["sp"])


class Rec:
    def __init__(self):
        self.calls = []
        self.pos = 0

    def op(self, *a, **k):
        self.calls.append(("op", a, k, None))

    def dma(self, *a, **k):
        r = [None]
        self.calls.append(("dma", a, k, r))
        return r

    def idma(self, *a, **k):
        r = [None]
        self.calls.append(("idma", a, k, r))
        return r

    def allgather(self, src, dst, deps):
        self.calls.append(("allgather", (src, dst, deps), {}, None))

    def pump(self, P, n):
        while n > 0 and self.pos < len(self.calls):
            m, a, k, r = self.calls[self.pos]
            self.pos += 1
            n -= 1
            if m == "allgather":
                P.allgather(a[0], a[1], [d[0] for d in a[2]])
            else:
                if "deps_extra" in k:
                    k = dict(k)
                t = getattr(P, m)(*a, **k)
                if r is not None:
                    r[0] = t


class Ctx:
    N = [0]

    def __init__(self, nc):
        self.nc = nc
        self.es = ExitStack()

    def sb(self, shape, dt, name=None):
        Ctx.N[0] += 1
        return self.es.enter_context(self.nc.sbuf_tensor("%s_%d" % (name or "t", Ctx.N[0]), list(shape), dt))

    def ps(self, shape, dt, name=None):
        Ctx.N[0] += 1
        return self.es.enter_context(self.nc.psum_tensor("%s_%d" % (name or "p", Ctx.N[0]), list(shape), dt))

    def __enter__(self):
        return self

    def __exit__(self, *a):
        return self.es.__exit__(*a)


def load_gbc(P, c, g_row):
    gbc = c.sb([128, D], F32, "gbc")
    b = Buf()
    P.dma("sp", gbc[:], g_row.partition_broadcast(128), writes=[b])
    return gbc, b


def load_gidx(P, c, gidx_d):
    gix = c.sb([128, 80], mybir.dt.int32, "gix")
    b = Buf()
    P.dma("sp", gix[:], gidx_d, writes=[b])
    return gix, b


def load_ident(P, c, ident_d):
    ident = c.sb([128, 128], BF16, "ident")
    b = Buf()
    P.dma("sp", ident[:], ident_d, writes=[b])
    return ident, b


def norm_transpose(P, c, x_src, row0, ntile, gbc, gbc_b, ident, ident_b, hT, hT_b, psT, psT_b, st):
    xt2, xt_b2, hb2, hb_b2, ss, ss_b, junk, junk_b = st

    def stage_a(tt):
        r0 = row0 + tt * 128
        xt = xt2[:, tt % 2]
        xt_b = xt_b2[tt % 2]
        hb = hb2[:, tt % 2]
        hb_b = hb_b2[tt % 2]
        sc = ss[:, tt % 2]
        sc_b = ss_b[tt % 2]
        P.dma("sp", xt, x_src[r0:r0 + 128, :], writes=[xt_b])
        P.op("act", "activation", out=junk[:], in_=xt, func=AF.Square, accum_out=sc[:, 0:1],
             reads=[xt_b], writes=[junk_b, sc_b])
        P.op("act", "activation", out=sc[:, 1:2], in_=sc[:, 0:1], func=AF.Sqrt, scale=1.0 / D, bias=sc[:, 3:4],
             reads=[sc_b], writes=[sc_b])
        P.op("dve", "reciprocal", out=sc[:, 2:3], in_=sc[:, 1:2], reads=[sc_b], writes=[sc_b])
        P.op("dve", "scalar_tensor_tensor", out=hb, in0=xt, scalar=sc[:, 2:3], in1=gbc[:],
             op0=ALU.mult, op1=ALU.mult, reads=[xt_b, sc_b, gbc_b], writes=[hb_b])

    def stage_b(tt):
        hb = hb2[:, tt % 2]
        hb_b = hb_b2[tt % 2]
        for g4 in range(4):
            pb = g4 % 2
            for j in range(4):
                kc = g4 * 4 + j
                P.op("pe", "transpose", out=psT[pb][:, j * 128:(j + 1) * 128], in_=hb[:, kc * 128:(kc + 1) * 128],
                     identity=ident[:], reads=[hb_b, ident_b], writes=[psT_b[pb]])
            src = psT[pb][:, 0:512].rearrange("p (a b) -> p a b", a=4)
            dst = hT[:, g4 * 4:(g4 + 1) * 4, tt * 128:(tt + 1) * 128]
            if g4 % 2 == 0:
                P.op("act", "activation", out=dst, in_=src, func=AF.Copy, reads=[psT_b[pb]], writes=[hT_b[tt]])
            else:
                P.op("dve", "tensor_copy", out=dst, in_=src, reads=[psT_b[pb]], writes=[hT_b[tt]])

    stage_a(0)
    for tt in range(ntile):
        if tt + 1 < ntile:
            stage_a(tt + 1)
        stage_b(tt)


def norm_state(P, c):
    xt = c.sb([128, 2, D], F32, "xt")
    hb = c.sb([128, 2, D], BF16, "hb")
    junk = c.sb([128, D], BF16, "junk")
    ss = c.sb([128, 2, 4], F32, "ss")
    ss_b = bufs(2)
    for b in range(2):
        P.op("dve", "memset", ap=ss[:, b, 3:4], constant=EPS, writes=[ss_b[b]])
    return (xt, bufs(2), hb, bufs(2), ss, ss_b, junk, Buf())


def ffn_phase(P, x_src, x_dst, g_row, Wg, Wu, Wd, ident_d):
    nc = P.nc
    TP = 1024
    NT = TP // 128
    with Ctx(nc) as c:
        hT = c.sb([128, 16, TP], BF16, "hT")
        aT = c.sb([128, 44, TP], BF16, "aT")
        wg = c.sb([128, 2, 16, 256], BF16, "wg")
        wu = c.sb([128, 2, 16, 256], BF16, "wu")
        wd = c.sb([128, 3, 4, 512], BF16, "wd")
        sg = c.sb([128, 2, 512], BF16, "sg")
        ps = [c.ps([128, 512], F32, "ps") for _ in range(8)]
        ps_b = bufs(8)
        psT = [ps[4][:].bitcast(BF16), ps[5][:].bitcast(BF16)]
        psT_b = [ps_b[4], ps_b[5]]
        gbc, gbc_b = load_gbc(P, c, g_row)
        ident, ident_b = load_ident(P, c, ident_d)
        st = norm_state(P, c)
        hT_b = bufs(NT)
        aT_b = [[Buf() for _ in range(2)] for _ in range(44)]
        wg_b, wu_b, wd_b, sg_b, xf_b = bufs(2), bufs(2), bufs(3), bufs(2), bufs(8)
        for p in range(TOK // TP):
            row0 = p * TP
            norm_transpose(P, c, x_src, row0, NT, gbc, gbc_b, ident, ident_b, hT, hT_b, psT, psT_b, st)
            for fg in range(22):
                b = fg % 2
                fs = slice(fg * 256, (fg + 1) * 256)
                P.dma("pool", wg[:, b], Wg[:, fs].rearrange("(kc p) f -> p kc f", p=128), writes=[wg_b[b]])
                P.dma("pool", wu[:, b], Wu[:, fs].rearrange("(kc p) f -> p kc f", p=128), writes=[wu_b[b]])
                for fc in range(2):
                    f = fg * 2 + fc
                    for th in range(2):
                        i2 = (f * 2 + th) % 2
                        G, Gb = ps[i2], ps_b[i2]
                        U, Ub = ps[2 + i2], ps_b[2 + i2]
                        hr = hT_b[th * 4:(th + 1) * 4]
                        for kc in range(16):
                            P.op("pe", "matmul", out=G[:], lhsT=wg[:, b, kc, fc * 128:(fc + 1) * 128],
                                 rhs=hT[:, kc, th * 512:(th + 1) * 512], start=(kc == 0), stop=(kc == 15),
                                 reads=[wg_b[b]] + hr, writes=[Gb])
                        for kc in range(16):
                            P.op("pe", "matmul", out=U[:], lhsT=wu[:, b, kc, fc * 128:(fc + 1) * 128],
                                 rhs=hT[:, kc, th * 512:(th + 1) * 512], start=(kc == 0), stop=(kc == 15),
                                 reads=[wu_b[b]] + hr, writes=[Ub])
                        P.op("act", "activation", out=sg[:, i2], in_=G[:], func=AF.Silu, reads=[Gb], writes=[sg_b[i2]])
                        P.op("dve", "tensor_tensor", out=aT[:, f, th * 512:(th + 1) * 512], in0=sg[:, i2], in1=U[:],
                             op=ALU.mult, reads=[sg_b[i2], Ub], writes=[aT_b[f][th]])
            nwd = 0
            xt2, xt_b2 = st[0], st[1]
            for dc in range(4):
                ds_ = slice(dc * 512, (dc + 1) * 512)
                for tt in range(NT):
                    r0 = row0 + tt * 128
                    xblk = xt2[:, tt // 4, (tt % 4) * 512:(tt % 4 + 1) * 512]
                    if dc == 0 and tt % 4 == 0:
                        P.dma("sp", xblk, x_src[r0:r0 + 128, ds_], writes=[xf_b[tt], xt_b2[tt // 4]])
                    else:
                        P.dma("sp", xblk, x_src[r0:r0 + 128, ds_], reads=[xt_b2[tt // 4]], writes=[xf_b[tt]])
                for fq in range(11):
                    wb = nwd % 3
                    nwd += 1
                    P.dma("pool", wd[:, wb], Wd[fq * 512:(fq + 1) * 512, ds_].rearrange("(fc p) d -> p fc d", p=128),
                          writes=[wd_b[wb]])
                    for tt in range(NT):
                        for fl in range(4):
                            f = fq * 4 + fl
                            P.op("pe", "matmul", out=ps[tt][:], lhsT=aT[:, f, tt * 128:(tt + 1) * 128],
                                 rhs=wd[:, wb, fl, :], start=(f == 0), stop=(f == 43),
                                 reads=[wd_b[wb], aT_b[f][tt // 4]], writes=[ps_b[tt]])
                for tt in range(NT):
                    xblk = xt2[:, tt // 4, (tt % 4) * 512:(tt % 4 + 1) * 512]
                    P.op("dve", "scalar_tensor_tensor", out=xblk, in0=ps[tt][:], scalar=0.5, in1=xblk,
                         op0=ALU.mult, op1=ALU.add, reads=[ps_b[tt]], writes=[xf_b[tt]])
                for tt in range(NT):
                    r0 = row0 + tt * 128
                    xblk = xt2[:, tt // 4, (tt % 4) * 512:(tt % 4 + 1) * 512]
                    P.dma("sp", x_dst[r0:r0 + 128, ds_], xblk, reads=[xf_b[tt], xt_b2[tt // 4]])
        P.barrier()
        P.emit()


def win_phase(P, x_src, g_row, Win, ident_d, zqk, zv, zr, zqk_g, zv_g, zr_g):
    nc = P.nc
    NT = TOK // 128
    with Ctx(nc) as c:
        hT = c.sb([128, 16, TOK], BF16, "hT")
        wt = c.sb([128, 3, 16, 256], BF16, "wt")
        wv = c.sb([128, 16, 1024], BF16, "wv")
        stf = c.sb([128, 3, TOK], F32, "stf")
        stb = c.sb([128, 3, TOK], BF16, "stb")
        vst = c.sb([128, 2, 1024], BF16, "vst")
        ps = [c.ps([128, 512], F32, "ps") for _ in range(8)]
        ps_b = bufs(8)
        psT = [ps[4][:].bitcast(BF16), ps[5][:].bitcast(BF16)]
        psT_b = [ps_b[4], ps_b[5]]
        gbc, gbc_b = load_gbc(P, c, g_row)
        ident, ident_b = load_ident(P, c, ident_d)
        st = norm_state(P, c)
        hT_b = bufs(NT)
        wt_b, stf_b, stb_b, vst_b = bufs(3), bufs(3), bufs(3), bufs(2)
        wv_b = Buf()
        for q4 in range(4):
            P.dma("pool", wv[:, :, q4 * 256:(q4 + 1) * 256],
                  Win[:, 2048 + q4 * 256:2048 + (q4 + 1) * 256].rearrange("(kc p) f -> p kc f", p=128), writes=[wv_b])
        norm_transpose(P, c, x_src, 0, NT, gbc, gbc_b, ident, ident_b, hT, hT_b, psT, psT_b, st)
        groups = []
        for j in range(4):
            groups.append((256 * j, [("qk", j, 0), ("qk", j, 1)]))
        for j in range(4):
            groups.append((1024 + 256 * j, [("qk", j, 2), ("qk", j, 3)]))
        groups.append("V")
        for m in range(2):
            groups.append((3072 + 256 * m, [("r", 2 * m, 0), ("r", 2 * m + 1, 0)]))
        for m in range(2):
            groups.append((3584 + 256 * m, [("r", 2 * m, 1), ("r", 2 * m + 1, 1)]))
        for m in range(2):
            groups.append((4096 + 256 * m, [("r", 2 * m, 2), ("r", 2 * m + 1, 2)]))
        cc0 = getattr(P, "ncc", 0)
        def do_v():
            vtoks = []
            for tt in range(NT):
                vb = tt % 2
                for half in range(2):
                    pi = 6 + (tt * 2 + half) % 2
                    for kc in range(16):
                        P.op("pe", "matmul", out=ps[pi][:], lhsT=hT[:, kc, tt * 128:(tt + 1) * 128],
                             rhs=wv[:, kc, half * 512:(half + 1) * 512], start=(kc == 0), stop=(kc == 15),
                             reads=[wv_b, hT_b[tt]], writes=[ps_b[pi]])
                    dst = vst[:, vb, half * 512:(half + 1) * 512]
                    if half == 0:
                        P.op("act", "activation", out=dst, in_=ps[pi][:], func=AF.Copy, reads=[ps_b[pi]],
                             writes=[vst_b[vb]])
                    else:
                        P.op("dve", "tensor_copy", out=dst, in_=ps[pi][:], reads=[ps_b[pi]], writes=[vst_b[vb]])
                vtoks.append(P.dma("sp", zv[:, :, tt, :].rearrange("j p c -> p j c"),
                                   vst[:, vb].rearrange("p (j c) -> p j c", j=4), reads=[vst_b[vb]]))
            for j in range(4):
                P.allgather(zv[j].rearrange("p b c -> p (b c)"), zv_g[j * 512:(j + 1) * 512, :], vtoks)

        fgroups = [g for g in groups if g != "V"]

        def load_w(k):
            if k < len(fgroups):
                c0 = fgroups[k][0]
                P.dma("pool", wt[:, k % 3], Win[:, c0:c0 + 256].rearrange("(kc p) f -> p kc f", p=128),
                      writes=[wt_b[k % 3]])

        load_w(0)
        load_w(1)
        nps = 0
        nst = 0
        gi = 0
        for grp in groups:
            if grp == "V":
                do_v()
                continue
            col0, chunks = grp
            b = gi % 3
            gi += 1
            load_w(gi + 1)
            toks = []
            for ci, (kind, dest, idx) in enumerate(chunks):
                sb_i = nst % 3
                nst += 1
                if kind == "qk":
                    stg, stg_b = stb, stb_b
                else:
                    stg, stg_b = stf, stf_b
                for th in range(4):
                    pi = nps % 4
                    nps += 1
                    for kc in range(16):
                        P.op("pe", "matmul", out=ps[pi][:], lhsT=wt[:, b, kc, ci * 128:(ci + 1) * 128],
                             rhs=hT[:, kc, th * 512:(th + 1) * 512], start=(kc == 0), stop=(kc == 15),
                             reads=[wt_b[b]] + hT_b[th * 4:(th + 1) * 4], writes=[ps_b[pi]])
                    dst = stg[:, sb_i, th * 512:(th + 1) * 512]
                    if th % 2 == 0:
                        P.op("act", "activation", out=dst, in_=ps[pi][:], func=AF.Copy, reads=[ps_b[pi]],
                             writes=[stg_b[sb_i]])
                    else:
                        P.op("dve", "tensor_copy", out=dst, in_=ps[pi][:], reads=[ps_b[pi]], writes=[stg_b[sb_i]])
                if kind == "qk":
                    toks.append(P.dma("sp", zqk[dest, idx], stg[:, sb_i], reads=[stg_b[sb_i]]))
                else:
                    t = P.dma("sp", zr[dest, idx], stg[:, sb_i], reads=[stg_b[sb_i]])
                    k = dest * 3 + idx
                    P.allgather(zr[dest, idx], zr_g[k * 512:(k + 1) * 512, :], [t])
            if chunks[0][0] == "qk":
                j, cp = chunks[0][1], chunks[0][2] // 2
                k = j * 2 + cp
                P.allgather(zqk[j, 2 * cp:2 * cp + 2].rearrange("c p t -> (c p) t"), zqk_g[k * 1024:(k + 1) * 1024, :], toks)
        P.barrier(cc=cc0 + 12)
        P.emit()
    return cc0 + 24


WOUT_ROW = lambda j, cc: (2 * j + cc) * 128 if cc < 2 else (1024 + j * 128 if cc == 2 else 1536 + j * 128)


def wout_phase(P, x_src, x_dst, yg, gidx_d, Wout, cc_all):
    nc = P.nc
    NT = TOK // 128
    with Ctx(nc) as c:
        yT = c.sb([128, 16, TOK], BF16, "yT")
        wo = c.sb([128, 16, D], BF16, "wo")
        xr = c.sb([128, 4, 512], F32, "xr")
        ps = [c.ps([128, 512], F32, "ps") for _ in range(8)]
        ps_b = bufs(8)
        yT_b, wo_b, xr_b = bufs(4), bufs(16), bufs(4)
        gix, gix_b = load_gidx(P, c, gidx_d)
        ydep = Buf()
        ydep.w = ("cc", cc_all)
        for j in range(4):
            for cc in range(4):
                r0 = WOUT_ROW(j, cc)
                P.dma("pool", wo[:, j * 4 + cc, :], Wout[r0:r0 + 128, :], writes=[wo_b[j * 4 + cc]])
        for j in range(4):
            for cc in range(4):
                P.idma(yT[:, j * 4 + cc, :], yg, gix[:, 32 + j * 4 + cc:33 + j * 4 + cc], reads=[gix_b, ydep],
                       writes=[yT_b[j]])
        tiles = [(tt, dc) for tt in range(NT) for dc in range(4)]

        def load_x(n):
            if n < len(tiles):
                tt, dc = tiles[n]
                P.dma("sp", xr[:, n % 4], x_src[tt * 128:(tt + 1) * 128, dc * 512:(dc + 1) * 512], writes=[xr_b[n % 4]])

        load_x(0)
        load_x(1)
        for n, (tt, dc) in enumerate(tiles):
            pi = n % 8
            xb = n % 4
            ds_ = slice(dc * 512, (dc + 1) * 512)
            for ch in range(16):
                P.op("pe", "matmul", out=ps[pi][:], lhsT=yT[:, ch, tt * 128:(tt + 1) * 128], rhs=wo[:, ch, ds_],
                     start=(ch == 0), stop=(ch == 15), reads=[yT_b[ch // 4], wo_b[ch]], writes=[ps_b[pi]])
            load_x(n + 2)
            r0 = tt * 128
            P.op("dve", "tensor_tensor", out=xr[:, xb], in0=ps[pi][:], in1=xr[:, xb], op=ALU.add,
                 reads=[ps_b[pi]], writes=[xr_b[xb]])
            P.dma("sp", x_dst[r0:r0 + 128, ds_], xr[:, xb], reads=[xr_b[xb]])
        P.barrier()
        P.emit()


def fnorm_phase(P, x_src, out, g_row):
    nc = P.nc
    with Ctx(nc) as c:
        gbc, gbc_b = load_gbc(P, c, g_row)
        xt = c.sb([128, 2, D], F32, "xt")
        ot = c.sb([128, 2, D], F32, "ot")
        ss = c.sb([128, 2, 4], F32, "ss")
        xt_b, ot_b, ss_b = bufs(2), bufs(2), bufs(2)
        for b in range(2):
            P.op("dve", "memset", ap=ss[:, b, 3:4], constant=EPS, writes=[ss_b[b]])
        for tt in range(TOK // 128):
            b = tt % 2
            r0 = tt * 128
            P.dma("sp", xt[:, b], x_src[r0:r0 + 128, :], writes=[xt_b[b]])
            P.op("act", "activation", out=ot[:, b], in_=xt[:, b], func=AF.Square, accum_out=ss[:, b, 0:1],
                 reads=[xt_b[b]], writes=[ot_b[b], ss_b[b]])
            P.op("act", "activation", out=ss[:, b, 1:2], in_=ss[:, b, 0:1], func=AF.Sqrt, scale=1.0 / D,
                 bias=ss[:, b, 3:4], reads=[ss_b[b]], writes=[ss_b[b]])
            P.op("dve", "reciprocal", out=ss[:, b, 2:3], in_=ss[:, b, 1:2], reads=[ss_b[b]], writes=[ss_b[b]])
            P.op("dve", "scalar_tensor_tensor", out=ot[:, b], in0=xt[:, b], scalar=ss[:, b, 2:3], in1=gbc[:],
                 op0=ALU.mult, op1=ALU.mult, reads=[xt_b[b], ss_b[b], gbc_b], writes=[ot_b[b]])
            P.dma("sp", out[r0:r0 + 128, :], ot[:, b], reads=[ot_b[b]])
        P.barrier()
        P.emit()


def rnnpool_body(P, c, zr_g, gix, gix_b, pvec_d, mats_d, invcnt_d, yout, yg, zdep):
    CW = 1024
    W = 16 + CW
    zr_h = zr_g.rearrange("r (h t) -> (r h) t", h=2)
    if True:
        pv = c.sb([128, 16], F32, "pv")
        mats = c.sb([128, 3, 128], BF16, "mats")
        sm = c.sb([128, 8], F32, "sm")
        xe = c.sb([128, 3 + CW], F32, "xe")
        gt = c.sb([128, CW], F32, "gt")
        u = c.sb([128, CW], F32, "u")
        ub = c.sb([128, CW], BF16, "ub")
        ra = c.sb([128, CW], F32, "ra")
        ii = c.sb([128, CW], F32, "ii")
        t1 = c.sb([128, CW], F32, "t1")
        h = c.sb([128, CW], F32, "h")
        ob = c.sb([128, 2, CW], BF16, "ob")
        pe_ = c.sb([128, W], F32, "pe")
        y2 = c.sb([128, W], F32, "y2")
        y4 = c.sb([128, W], F32, "y4")
        y8 = c.sb([128, W], F32, "y8")
        y16 = c.sb([128, W], F32, "y16")
        ic = c.sb([128, CW], F32, "ic")
        sl = c.sb([128, CW], F32, "sl")
        db = c.sb([128, CW], BF16, "db")
        ps = [c.ps([128, 512], F32, "ps") for _ in range(2)]
        ps_b = bufs(2)
        (pv_b, mats_b, sm_b, xe_b, gt_b, u_b, ub_b, ra_b, ii_b, t1_b, h_b, pe_b, y2_b, y4_b, y8_b, y16_b,
         ic_b, sl_b, db_b) = bufs(19)
        ob_b = bufs(2)
        P.dma("sp", pv[:], pvec_d, writes=[pv_b])
        P.dma("pool", mats[:], mats_d.rearrange("m p q -> p m q"), writes=[mats_b])
        P.op("dve", "memset", ap=sm[:, 3:4], constant=1.0, writes=[sm_b])
        P.op("act", "activation", out=sm[:, 0:1], in_=pv[:, 7:8], func=AF.Exp, scale=-1.0, reads=[pv_b, sm_b], writes=[sm_b])
        P.op("act", "activation", out=sm[:, 1:2], in_=sm[:, 0:1], func=AF.Ln, bias=sm[:, 3:4], reads=[sm_b], writes=[sm_b])
        P.op("dve", "tensor_scalar", out=sm[:, 2:3], in0=sm[:, 1:2], scalar1=-8.0, scalar2=None, op0=ALU.mult,
             reads=[sm_b], writes=[sm_b])
        npi = 0
        ytoks = []
        for ci in range(8):
            i = ci
            src_i, hh = ci // 2, ci % 2
            if i == 0:
                P.op("dve", "memset", ap=xe[:, 0:3], constant=0.0, writes=[xe_b])
                P.op("dve", "memset", ap=pe_[:, 0:16], constant=0.0, writes=[pe_b])
            else:
                P.op("dve", "tensor_copy", out=xe[:, 0:3], in_=xe[:, CW:CW + 3], reads=[xe_b], writes=[xe_b])
                P.op("dve", "tensor_copy", out=pe_[:, 0:16], in_=pe_[:, CW:CW + 16], reads=[pe_b], writes=[pe_b])
            def gcol(cc_):
                k_ = 48 + (src_i * 3 + cc_) * 2 + hh
                return gix[:, k_:k_ + 1]
            P.idma(xe[:, 3:3 + CW], zr_h, gcol(1), reads=[gix_b, zdep], writes=[xe_b])
            P.idma(gt[:], zr_h, gcol(0), reads=[gix_b, zdep], writes=[gt_b])
            P.idma(pe_[:, 16:W], zr_h, gcol(2), reads=[gix_b, zdep], writes=[pe_b])
            P.dma("sp", ic[:], invcnt_d[:, i * CW:(i + 1) * CW], writes=[ic_b])
            P.op("dve", "tensor_scalar", out=u[:], in0=xe[:, 0:CW], scalar1=pv[:, 0:1], scalar2=pv[:, 4:5],
                 op0=ALU.mult, op1=ALU.add, reads=[xe_b, pv_b], writes=[u_b])
            for k in range(1, 4):
                P.op("dve", "scalar_tensor_tensor", out=u[:], in0=xe[:, k:k + CW], scalar=pv[:, k:k + 1], in1=u[:],
                     op0=ALU.mult, op1=ALU.add, reads=[xe_b, pv_b], writes=[u_b])
            P.op("pool", "tensor_copy", out=ub[:], in_=u[:], reads=[u_b], writes=[ub_b])
            for q in range(CW // 512):
                qs = slice(q * 512, (q + 1) * 512)
                pa = npi % 2
                pb = (npi + 1) % 2
                npi += 2
                P.op("pe", "matmul", out=ps[pa][:], lhsT=mats[:, 0, :], rhs=ub[:, qs], start=True, stop=True,
                     reads=[mats_b, ub_b], writes=[ps_b[pa]])
                P.op("pe", "matmul", out=ps[pb][:], lhsT=mats[:, 1, :], rhs=ub[:, qs], start=True, stop=True,
                     reads=[mats_b, ub_b], writes=[ps_b[pb]])
                P.op("act", "activation", out=ra[:, qs], in_=ps[pa][:], func=AF.Sigmoid, bias=pv[:, 5:6],
                     reads=[ps_b[pa], pv_b], writes=[ra_b])
                P.op("act", "activation", out=ii[:, qs], in_=ps[pb][:], func=AF.Sigmoid, bias=pv[:, 6:7],
                     reads=[ps_b[pb], pv_b], writes=[ii_b])
            P.op("act", "activation", out=ra[:], in_=ra[:], func=AF.Exp, scale=sm[:, 2:3], reads=[sm_b], writes=[ra_b])
            P.op("pool", "tensor_tensor", out=t1[:], in0=ra[:], in1=ra[:], op=ALU.mult, reads=[ra_b], writes=[t1_b])
            P.op("dve", "tensor_scalar", out=t1[:], in0=t1[:], scalar1=-1.0, scalar2=1.0, op0=ALU.mult, op1=ALU.add,
                 reads=[], writes=[t1_b])
            P.op("act", "activation", out=t1[:], in_=t1[:], func=AF.Sqrt, reads=[], writes=[t1_b])
            P.op("dve", "tensor_tensor", out=ii[:], in0=ii[:], in1=u[:], op=ALU.mult, reads=[u_b], writes=[ii_b])
            P.op("dve", "tensor_tensor", out=ii[:], in0=ii[:], in1=t1[:], op=ALU.mult, reads=[t1_b], writes=[ii_b])
            if i > 0:
                P.op("dve", "tensor_copy", out=sm[:, 4:5], in_=h[:, CW - 1:CW], reads=[h_b], writes=[sm_b])
            P.op("dve", "tensor_tensor_scan", out=h[:], data0=ra[:], data1=ii[:],
                 initial=(0.0 if i == 0 else sm[:, 4:5]), op0=ALU.mult, op1=ALU.add,
                 reads=[ra_b, ii_b, sm_b], writes=[h_b])
            P.op("pool", "tensor_tensor", out=t1[:], in0=gt[:], in1=gt[:], op=ALU.mult, reads=[gt_b], writes=[t1_b])
            P.op("dve", "tensor_scalar", out=t1[:], in0=t1[:], scalar1=0.044715, scalar2=1.0, op0=ALU.mult, op1=ALU.add,
                 reads=[], writes=[t1_b])
            P.op("pool", "tensor_tensor", out=t1[:], in0=t1[:], in1=gt[:], op=ALU.mult, reads=[gt_b], writes=[t1_b])
            P.op("act", "activation", out=t1[:], in_=t1[:], func=AF.Sigmoid, scale=1.5957691216057308, reads=[],
                 writes=[t1_b])
            P.op("dve", "tensor_tensor", out=t1[:], in0=t1[:], in1=gt[:], op=ALU.mult, reads=[gt_b], writes=[t1_b])
            P.op("dve", "tensor_tensor", out=ob[:, 0], in0=t1[:], in1=h[:], op=ALU.mult, reads=[t1_b, h_b],
                 writes=[ob_b[0]])
            ytoks.append(P.dma("sp", yout[src_i, 2, :, hh * CW:(hh + 1) * CW], ob[:, 0], reads=[ob_b[0]]))
            P.op("pool", "tensor_tensor", out=y2[:, 1:W], in0=pe_[:, 1:W], in1=pe_[:, 0:W - 1], op=ALU.add,
                 reads=[pe_b], writes=[y2_b])
            P.op("pool", "tensor_tensor", out=y4[:, 3:W], in0=y2[:, 3:W], in1=y2[:, 1:W - 2], op=ALU.add,
                 reads=[y2_b], writes=[y4_b])
            P.op("pool", "tensor_tensor", out=y8[:, 7:W], in0=y4[:, 7:W], in1=y4[:, 3:W - 4], op=ALU.add,
                 reads=[y4_b], writes=[y8_b])
            P.op("pool", "tensor_tensor", out=y16[:, 15:W], in0=y8[:, 15:W], in1=y8[:, 7:W - 8], op=ALU.add,
                 reads=[y8_b], writes=[y16_b])
            P.op("dve", "tensor_scalar", out=sl[:], in0=y2[:, 16:W], scalar1=pv[:, 9:10], scalar2=None, op0=ALU.mult,
                 reads=[y2_b, pv_b], writes=[sl_b])
            for k, (yy, yy_b) in enumerate(((y4, y4_b), (y8, y8_b), (y16, y16_b))):
                P.op("dve", "scalar_tensor_tensor", out=sl[:], in0=yy[:, 16:W], scalar=pv[:, 10 + k:11 + k], in1=sl[:],
                     op0=ALU.mult, op1=ALU.add, reads=[yy_b, pv_b], writes=[sl_b])
            P.op("pool", "tensor_tensor", out=sl[:], in0=sl[:], in1=ic[:], op=ALU.mult, reads=[ic_b], writes=[sl_b])
            P.op("dve", "tensor_tensor", out=db[:], in0=sl[:], in1=pe_[:, 16:W], op=ALU.subtract, reads=[sl_b, pe_b],
                 writes=[db_b])
            for q in range(CW // 512):
                qs = slice(q * 512, (q + 1) * 512)
                pa = npi % 2
                npi += 1
                P.op("pe", "matmul", out=ps[pa][:], lhsT=mats[:, 2, :], rhs=db[:, qs], start=True, stop=True,
                     reads=[mats_b, db_b], writes=[ps_b[pa]])
                P.op("dve", "tensor_scalar", out=ob[:, 1, qs], in0=ps[pa][:], scalar1=pv[:, 8:9], scalar2=None,
                     op0=ALU.mult, reads=[ps_b[pa], pv_b], writes=[ob_b[1]])
            ytoks.append(P.dma("sp", yout[src_i, 3, :, hh * CW:(hh + 1) * CW], ob[:, 1], reads=[ob_b[1]]))
            if hh == 1:
                P.allgather(yout[src_i, 2:4].rearrange("c p t -> (c p) t"),
                            yg[(src_i * 2 + 1) * 1024:(src_i * 2 + 2) * 1024, :], ytoks[-4:])


def attn_phase(P, zqk_g, zv_g, zr_g, gidx_d, masks_d, tri_d, pvec_d, mats_d, invcnt_d, yout, yg, cc_all):
    nc = P.nc
    SCALE = 128.0 ** -0.5
    NQ = S // 512
    with Ctx(nc) as c:
        qT = c.sb([128, 2, S], BF16, "qT")
        kT = c.sb([128, 2, S], BF16, "kT")
        vv = c.sb([128, 4, 16, 256], BF16, "vv")
        mk = c.sb([128, 4, 512], F32, "mk")
        tc_ = c.sb([128, 2, 128], BF16, "tc")
        one = c.sb([128, 1], F32, "one")
        eb = c.sb([128, 2, 3, 512], F32, "eb")
        spb = c.sb([128, 2, 2, 512], BF16, "spb")
        xb = c.sb([128, 2, 2, 512], F32, "xb")
        ab = c.sb([128, 2, 2, 512], BF16, "ab")
        ys = c.sb([128, 2, 2, 512], BF16, "ys")
        psS2 = c.ps([128, 2, 512], F32, "psS")
        psT2 = c.ps([128, 2, 512], F32, "psT")
        psS = [psS2[:, 0, :], psS2[:, 1, :]]
        psT = [psT2[:, 0, :], psT2[:, 1, :]]
        psO = [[c.ps([128, 512], F32, "psO")] * 2 for _ in range(2)]
        psS_b, psT_b = bufs(2), bufs(2)
        psO_b = [[Buf()] * 2, [Buf()] * 2]
        q_b, k_b, v_b = [bufs(4), bufs(4)], [bufs(4), bufs(4)], bufs(4)
        mk_b, tc_b, one_b = Buf(), Buf(), Buf()
        eb_b = [bufs(3), bufs(3)]
        sp_b = [bufs(2), bufs(2)]
        x_b = [bufs(2), bufs(2)]
        a_b = [bufs(2), bufs(2)]
        ys_b = [bufs(2), bufs(2)]
        P.dma("sp", mk[:], masks_d, writes=[mk_b])
        P.dma("sp", tc_[:], tri_d, writes=[tc_b])
        P.op("dve", "memset", ap=one[:], constant=1.0, writes=[one_b])
        zt = c.sb([128, 512], BF16, "zt")
        zt_b = Buf()
        P.op("dve", "memset", ap=zt[:], constant=0.0, writes=[zt_b])
        gix, gix_b = load_gidx(P, c, gidx_d)
        for i in range(4):
            first = [] if i == 0 else [q_b[0][0], q_b[1][0], k_b[0][0], k_b[1][0], v_b[0]]
            for hd in range(2):
                P.idma(qT[:, hd, i * TOK:(i + 1) * TOK], zqk_g, gix[:, i * 4 + hd:i * 4 + hd + 1],
                       reads=[gix_b] + first, writes=[q_b[hd][i]])
                P.idma(kT[:, hd, i * TOK:(i + 1) * TOK], zqk_g, gix[:, i * 4 + 2 + hd:i * 4 + 3 + hd],
                       reads=[gix_b] + first, writes=[k_b[hd][i]])
            P.idma(vv[:, i].rearrange("p b c -> p (b c)"), zv_g, gix[:, 28 + i:29 + i], reads=[gix_b] + first,
                   writes=[v_b[i]])
        zdep = Buf()
        zdep.w = ("cc", cc_all)
        rec = Rec()
        rnnpool_body(rec, c, zr_g, gix, gix_b, pvec_d, mats_d, invcnt_d, yout, yg, zdep)
        atoks = {}
        blocks = []
        for qi in range(NQ):
            for kb in range(4 * qi + 3, -1, -1):
                blocks.append((qi, kb))
        NBLK = len(blocks)

        def lo_of(n):
            qi, kb = blocks[n]
            return max(kb - 4 * qi, 0) * 128

        def stage1_qk(cn, n):
            qi, kb = blocks[n]
            lo = lo_of(n)
            P.op("pe", "matmul", out=psS2[:, cn, lo:512], lhsT=kT[:, cn, kb * 128:(kb + 1) * 128],
                 rhs=qT[:, cn, qi * 512 + lo:(qi + 1) * 512], start=True, stop=True,
                 reads=[k_b[cn][kb // 16], q_b[cn][qi // 4]],
                 writes=[psS_b[cn]])

        def stage1_act(n):
            qi, kb = blocks[n]
            e_i = n % 3
            s_i = n % 2
            lo = lo_of(n)
            P.op("act", "activation", out=eb[:, :, e_i, lo:512], in_=psS2[:, :, lo:512], func=AF.Exp, scale=SCALE,
                 reads=[psS_b[0], psS_b[1]], writes=[eb_b[0][e_i], eb_b[1][e_i]])
            r = kb - 4 * qi
            if r >= 0:
                for cn in range(2):
                    P.op("dve", "tensor_tensor", out=eb[:, cn, e_i, lo:512], in0=eb[:, cn, e_i, lo:512],
                         in1=mk[:, r, lo:512], op=ALU.mult, reads=[mk_b], writes=[eb_b[cn][e_i]])
            P.op("act", "activation", out=spb[:, :, s_i, lo:512], in_=eb[:, :, e_i, lo:512], func=AF.Ln, bias=1.0,
                 reads=[eb_b[0][e_i], eb_b[1][e_i]], writes=[sp_b[0][s_i], sp_b[1][s_i]])

        def act_x2(n):
            lo = lo_of(n)
            P.op("act", "activation", out=xb[:, :, n % 2, lo:512], in_=psT2[:, :, lo:512], func=AF.Exp, scale=-1.0,
                 reads=[psT_b[0], psT_b[1]], writes=[x_b[0][n % 2], x_b[1][n % 2]])

        def pe_tri(cn, n):
            qi, kb = blocks[n]
            first = (kb == 4 * qi + 3)
            lo = lo_of(n)
            if first:
                P.op("pe", "matmul", out=psT2[:, cn, :], lhsT=tc_[:, 0, :], rhs=zt[:], start=True, stop=False,
                     skip_group_check=True, reads=[tc_b, zt_b], writes=[psT_b[cn]])
            P.op("pe", "matmul", out=psT2[:, cn, lo:512], lhsT=tc_[:, 0, :], rhs=spb[:, cn, n % 2, lo:512], start=False,
                 stop=False, skip_group_check=True, reads=[tc_b, sp_b[cn][n % 2]], writes=[psT_b[cn]])

        def act_x(cn, n):
            lo = lo_of(n)
            P.op("act", "activation", out=xb[:, cn, n % 2, lo:512], in_=psT[cn][:, lo:512], func=AF.Exp, scale=-1.0,
                 reads=[psT_b[cn]], writes=[x_b[cn][n % 2]])

        def pe_comp(cn, n):
            qi, kb = blocks[n]
            if kb == 0:
                return
            lo = lo_of(n)
            P.op("pe", "matmul", out=psT2[:, cn, lo:512], lhsT=tc_[:, 1, :], rhs=spb[:, cn, n % 2, lo:512], start=False,
                 stop=False, skip_group_check=True, reads=[tc_b, sp_b[cn][n % 2]], writes=[psT_b[cn]])

        def dve_a(cn, n):
            lo = lo_of(n)
            P.op("dve", "tensor_tensor", out=ab[:, cn, n % 2, lo:512], in0=eb[:, cn, n % 3, lo:512],
                 in1=xb[:, cn, n % 2, lo:512], op=ALU.mult,
                 reads=[eb_b[cn][n % 3], x_b[cn][n % 2]], writes=[a_b[cn][n % 2]])

        def pe_av(cn, n):
            qi, kb = blocks[n]
            first = (kb == 4 * qi + 3)
            o_i = qi % 2
            lo = lo_of(n)
            if first:
                P.op("pe", "matmul", out=psO[cn][o_i][:], lhsT=tc_[:, 0, :], rhs=zt[:], start=True, stop=False,
                     skip_group_check=True, reads=[tc_b, zt_b], writes=[psO_b[cn][o_i]])
            P.op("pe", "matmul", out=psO[cn][o_i][:, lo:512], lhsT=vv[:, kb // 16, kb % 16, cn * 128:(cn + 1) * 128],
                 rhs=ab[:, cn, n % 2, lo:512], start=False, skip_group_check=True,
                 stop=(kb == 0), reads=[v_b[kb // 16], a_b[cn][n % 2]], writes=[psO_b[cn][o_i]])
            if kb == 0:
                y_i = qi % 2
                if cn == 0:
                    P.op("act", "activation", out=ys[:, cn, y_i], in_=psO[cn][o_i][:], func=AF.Copy,
                         reads=[psO_b[cn][o_i]], writes=[ys_b[cn][y_i]])
                else:
                    P.op("dve", "tensor_copy", out=ys[:, cn, y_i], in_=psO[cn][o_i][:], reads=[psO_b[cn][o_i]],
                         writes=[ys_b[cn][y_i]])
                t = P.dma("sp", yout[qi // 4, cn, :, (qi % 4) * 512:(qi % 4 + 1) * 512], ys[:, cn, y_i],
                          reads=[ys_b[cn][y_i]])
                atoks.setdefault(qi // 4, []).append(t)
                if len(atoks[qi // 4]) == 8:
                    i = qi // 4
                    P.allgather(yout[i, 0:2].rearrange("c p t -> (c p) t"), yg[(i * 2) * 1024:(i * 2 + 1) * 1024, :],
                                atoks[i])

        for cn in range(2):
            stage1_qk(cn, 0)
        stage1_act(0)
        for n in range(NBLK):
            if n >= 140:
                rec.pump(P, 2)
            for cn in range(2):
                pe_tri(cn, n)
            act_x2(n)
            if n + 1 < NBLK:
                for cn in range(2):
                    stage1_qk(cn, n + 1)
                stage1_act(n + 1)
            for cn in range(2):
                pe_comp(cn, n)
            for cn in range(2):
                dve_a(cn, n)
            if n >= 1:
                for cn in range(2):
                    pe_av(cn, n - 1)
        for cn in range(2):
            pe_av(cn, NBLK - 1)
        rec.pump(P, 1 << 30)
        P.barrier(cc=0)
        P.emit()
    return P.ncc


def _dram_in(nc, name, shape, dt=F32):
    return nc.dram_tensor(name, list(shape), dt, kind="ExternalInput").ap()


def _dram_out(nc, name, shape, dt=F32):
    return nc.dram_tensor(name, list(shape), dt, kind="ExternalOutput").ap()


def _dram_tmp(nc, name, shape, dt=F32):
    return nc.dram_tensor(name, list(shape), dt, kind="Internal").ap()


def build_fused():
    nc = bass.Bass("TRN2", target_bir_lowering=False)
    x = _dram_in(nc, "x", [TOK, D])
    ident = _dram_in(nc, "ident", [128, 128], BF16)
    gidx = _dram_in(nc, "gidx", [128, 80], mybir.dt.int32)
    invcnt = _dram_in(nc, "invcnt", [128, S])
    masks = _dram_in(nc, "masks", [128, 4, 512])
    tri = _dram_in(nc, "tri", [128, 2, 128], BF16)
    gfin = _dram_in(nc, "gfin", [D])
    out = _dram_out(nc, "out", [TOK, D])
    W = []
    for l in range(DEPTH):
        w = {}
        for tag in ("f1", "f2"):
            w[tag] = (_dram_in(nc, "%s_g%d" % (tag, l), [D]), _dram_in(nc, "%s_gate%d" % (tag, l), [D, DFF]),
                      _dram_in(nc, "%s_up%d" % (tag, l), [D, DFF]), _dram_in(nc, "%s_down%d" % (tag, l), [DFF, D]))
        w["gmix"] = _dram_in(nc, "gmix%d" % l, [D])
        w["win"] = _dram_in(nc, "win%d" % l, [D, DIN])
        w["wout"] = _dram_in(nc, "wout%d" % l, [D, D])
        w["pvec"] = _dram_in(nc, "pvec%d" % l, [128, 16])
        w["mats"] = _dram_in(nc, "mats%d" % l, [3, 128, 128])
        W.append(w)
    xa = _dram_tmp(nc, "xa", [TOK, D])
    xb = _dram_tmp(nc, "xb", [TOK, D])
    xc = _dram_tmp(nc, "xc", [TOK, D])
    zqk = _dram_tmp(nc, "zqk", [4, 4, 128, TOK], BF16)
    zv = _dram_tmp(nc, "zv", [4, 128, 16, 256], BF16)
    zr = _dram_tmp(nc, "zr", [4, 3, 128, TOK], F32)
    ys = _dram_tmp(nc, "ys", [4, 4, 128, TOK], BF16)
    zqk_g = _dram_tmp(nc, "zqk_g", [4 * 4 * 4 * 128, TOK], BF16)
    zv_g = _dram_tmp(nc, "zv_g", [4 * 4 * 128, 16 * 256], BF16)
    zr_g = _dram_tmp(nc, "zr_g", [4 * 4 * 3 * 128, TOK], F32)
    yg = _dram_tmp(nc, "yg", [4 * 4 * 4 * 128, TOK], BF16)
    with ExitStack() as es:
        P = Prog(nc, es)
        cur = x
        for l in range(DEPTH):
            w = W[l]
            ffn_phase(P, cur, xa, w["f1"][0], w["f1"][1], w["f1"][2], w["f1"][3], ident)
            cc_all = win_phase(P, xa, w["gmix"], w["win"], ident, zqk, zv, zr, zqk_g, zv_g, zr_g)
            cc_y = attn_phase(P, zqk_g, zv_g, zr_g, gidx, masks, tri, w["pvec"], w["mats"], invcnt, ys, yg, cc_all)
            wout_phase(P, xa, xb, yg, gidx, w["wout"], cc_y)
            ffn_phase(P, xb, xc, w["f2"][0], w["f2"][1], w["f2"][2], w["f2"][3], ident)
            cur = xc
        fnorm_phase(P, cur, out, gfin)
    return nc


_CACHE = {}


def _consts():
    ident = np.eye(128, dtype=np.float32).astype(ml_dtypes.bfloat16)
    s = np.arange(128)[:, None]
    t = np.arange(512)[None, :]
    masks = np.stack([(t > r * 128 + s).astype(np.float32) for r in range(4)], axis=1)
    jj = np.arange(128)[:, None]
    ss = np.arange(128)[None, :]
    tri = (jj >= ss).astype(np.float32)
    comp = 1.0 - tri
    tri2 = np.stack([tri, comp], axis=1).astype(ml_dtypes.bfloat16)
    pos = np.arange(S, dtype=np.float32) + 1.0
    invcnt = [np.ascontiguousarray(np.broadcast_to((1.0 / np.minimum(pos, float(w)))[None, :], (128, S)))
              .astype(np.float32) for w in (2, 4, 8, 16)]
    return ident, masks, tri2, invcnt


def _gidx(r):
    g = np.zeros((128, 80), np.int32)
    p = np.arange(128)
    for i in range(4):
        for c in range(4):
            g[:, i * 4 + c] = (r * 2 + c // 2) * 1024 + i * 256 + (c % 2) * 128 + p
        for c in range(3):
            g[:, 16 + i * 3 + c] = (r * 3 + c) * 512 + i * 128 + p
        g[:, 28 + i] = r * 512 + i * 128 + p
        for c in range(3):
            for h in range(2):
                g[:, 48 + (i * 3 + c) * 2 + h] = 2 * ((r * 3 + c) * 512 + i * 128 + p) + h
    for j in range(4):
        for c in range(4):
            g[:, 32 + j * 4 + c] = (r * 2 + c // 2) * 1024 + j * 256 + (c % 2) * 128 + p
    return g


def _mix_params(inp, l, j):
    pv = np.zeros((128, 16), np.float32)
    sl = slice(j * 128, (j + 1) * 128)
    pv[:, 0:4] = inp["conv_w"][l][:, sl].T
    pv[:, 4] = inp["conv_b"][l][sl]
    pv[:, 5] = inp["rg_b_a"][l][sl]
    pv[:, 6] = inp["rg_b_x"][l][sl]
    pv[:, 7] = inp["rg_lambda"][l][sl]
    pv[:, 8] = inp["pool_scale"][l][sl]
    pv[:, 9 + j] = 1.0
    mats = np.stack([inp["rg_w_a"][l][j], inp["rg_w_x"][l][j], inp["pool_w"][l][j]], axis=0)
    return pv, np.ascontiguousarray(mats, dtype=np.float32)


def kernel(**inp):
    inp = {k: np.asarray(v) for k, v in inp.items()}
    ident, masks, tri2, invcnt = _consts()
    xs = inp["x"].reshape(NCORE, TOK, D)
    if "nc" not in _CACHE:
        _CACHE["nc"] = build_fused()
    nc = _CACHE["nc"]
    base = dict(ident=ident, masks=masks, tri=tri2, gfin=inp["norm_final"])
    for l in range(DEPTH):
        for tag, which in (("f1", "ffn1"), ("f2", "ffn2")):
            base["%s_g%d" % (tag, l)] = inp["norm_" + which][l]
            base["%s_gate%d" % (tag, l)] = inp[which + "_gate"][l]
            base["%s_up%d" % (tag, l)] = inp[which + "_up"][l]
            base["%s_down%d" % (tag, l)] = inp[which + "_down"][l]
        base["gmix%d" % l] = inp["norm_mix"][l]
        base["win%d" % l] = inp["w_in"][l]
        base["wout%d" % l] = inp["w_out"][l]
    maps = []
    for c in range(NCORE):
        r = c % 4
        m = dict(base, x=xs[c], gidx=_gidx(r), invcnt=invcnt[r])
        for l in range(DEPTH):
            pv, mats = _mix_params(inp, l, r)
            m["pvec%d" % l] = pv
            m["mats%d" % l] = mats
        maps.append(m)
    res = run_bass_kernel_spmd(nc, maps, core_ids=list(range(NCORE)))
    out = np.stack([r["out"] for r in res.results], axis=0)
    return out.reshape(NB, S, D).astype(np.float32)
```

```python
import numpy as np
import ml_dtypes
from contextlib import ExitStack

import concourse.bass as bass
import concourse.mybir as mybir
from concourse.bass_utils import run_bass_kernel_spmd

F32 = mybir.dt.float32
BF16 = mybir.dt.bfloat16
ALU = mybir.AluOpType
AF = mybir.ActivationFunctionType

D = 2048
DFF = 5632
S = 8192
NB = 2
DEPTH = 2
TOK = 2048
DIN = 4608
EPS = 1e-6
NCORE = 8


class Buf:
    __slots__ = ("w", "r")

    def __init__(self):
        self.w = None
        self.r = {}


def bufs(n):
    return [Buf() for _ in range(n)]


class Prog:
    ENG = ("pe", "act", "dve", "pool", "sp")
    NDMA = 24

    def __init__(self, nc, es):
        self.nc = nc
        self.ops = {e: [] for e in self.ENG}
        self.cnt = {"pe": 0, "act": 0, "dve": 0, "pool": 0}
        self.seen = {e: {} for e in self.ENG}
        self.dma_n = {"sp": 0, "pool": 0}
        self.dma_last = {}
        self.sems = {}
        for k in ("pe", "act", "dve", "pool", "cc"):
            self.sems[k] = es.enter_context(nc.semaphore("s_" + k))
        for q in ("sp", "pool"):
            for i in range(self.NDMA):
                self.sems[("d", q, i)] = es.enter_context(nc.semaphore("d_%s%d" % (q, i)))

    def _deps(self, reads, writes):
        d = []
        for b in reads:
            if b.w is not None:
                d.append(b.w)
        for b in writes:
            if b.w is not None:
                d.append(b.w)
            d.extend(b.r.items())
        return d

    def _waits(self, eng, deps):
        need = {}
        for k, v in deps:
            if v > need.get(k, 0):
                need[k] = v
        out = []
        seen = self.seen[eng]
        for k, v in need.items():
            if k == eng and eng == "pe":
                continue
            if seen.get(k, 0) >= v:
                continue
            seen[k] = v
            out.append((k, v))
        return out

    def _mark(self, tok, reads, writes):
        for b in reads:
            if b.r.get(tok[0], 0) < tok[1]:
                b.r[tok[0]] = tok[1]
        for b in writes:
            b.w = tok
            b.r = {}

    def op(self, eng, name, reads=(), writes=(), **kw):
        waits = self._waits(eng, self._deps(reads, writes))
        self.cnt[eng] += 1
        tok = (eng, self.cnt[eng])
        self.ops[eng].append((waits, name, kw, tok, 1))
        self._mark(tok, reads, writes)
        return tok

    def dma(self, q, out, in_, reads=(), writes=(), **kw):
        n = self.dma_n[q]
        self.dma_n[q] += 1
        k = ("d", q, n % self.NDMA)
        val = 16 * (n // self.NDMA + 1)
        deps = self._deps(reads, writes)
        if val > 16:
            deps.append((k, val - 16))
        waits = self._waits(q, deps)
        tok = (k, val)
        kw = dict(kw)
        kw["out"] = out
        kw["in_"] = in_
        self.ops[q].append((waits, "dma_start", kw, tok, 16))
        self.dma_last[k] = val
        self._mark(tok, reads, writes)
        return tok

    def idma(self, out, in_, idx, reads=(), writes=()):
        return self.dma("pool", out, in_, reads=reads, writes=writes, _idx=idx)

    def allgather(self, src, dst, deps):
        waits = self._waits("pool", list(deps))
        self.ncc = getattr(self, "ncc", 0) + 1
        tok = ("cc", self.ncc)
        kw = dict(kind="AllGather", op=ALU.bypass, replica_groups=[[0, 1, 2, 3], [4, 5, 6, 7]],
                  ins=[src.opt()], outs=[dst.opt()])
        self.ops["pool"].append((waits, "collective_compute", kw, tok, None))
        self.dma_last["cc"] = self.ncc
        return tok

    def barrier(self, cc=None):
        deps = [(k, v) for k, v in self.cnt.items() if v > 0]
        deps += [(k, v) for k, v in self.dma_last.items() if not (k == "cc" and cc is not None)]
        if cc is not None and cc > 0:
            deps.append(("cc", cc))
        for e in self.ENG:
            waits = self._waits(e, deps)
            if waits:
                self.ops[e].append((waits, None, None, None, 0))

    def _run(self, e, lst):
        for waits, name, kw, tok, inc in lst:
            for k, v in waits:
                e.wait_ge(self.sems[k], v)
            if name is None:
                continue
            if name == "dma_start" and "_idx" in kw:
                kw = dict(kw)
                idx = kw.pop("_idx")
                ins = e.indirect_dma_start(out=kw["out"], out_offset=None, in_=kw["in_"],
                                           in_offset=bass.IndirectOffsetOnAxis(ap=idx, axis=0))
            else:
                ins = getattr(e, name)(**kw)
            if inc is None:
                ins.then_inc(self.sems[tok[0]])
            else:
                ins.then_inc(self.sems[tok[0]], inc)

    def emit(self):
        ops = self.ops
        self.ops = {e: [] for e in self.ENG}
        with self.nc.Block() as block:
            @block.tensor
            def _(e):
                self._run(e, ops["pe"])

            @block.scalar
            def _(e):
                self._run(e, ops["act"])

            @block.vector
            def _(e):
                self._run(e, ops["dve"])

            @block.gpsimd
            def _(e):
                self._run(e, ops["pool"])

            @block.sync
            def _(e):
                self._run(e, ops["sp"])


class Rec:
    def __init__(self):
        self.calls = []
        self.pos = 0

    def op(self, *a, **k):
        self.calls.append(("op", a, k, None))

    def dma(self, *a, **k):
        r = [None]
        self.calls.append(("dma", a, k, r))
        return r

    def idma(self, *a, **k):
        r = [None]
        self.calls.append(("idma", a, k, r))
        return r

    def allgather(self, src, dst, deps):
        self.calls.append(("allgather", (src, dst, deps), {}, None))

    def pump(self, P, n):
        while n > 0 and self.pos < len(self.calls):
            m, a, k, r = self.calls[self.pos]
            self.pos += 1
            n -= 1
            if m == "allgather":
                P.allgather(a[0], a[1], [d[0] for d in a[2]])
            else:
                if "deps_extra" in k:
                    k = dict(k)
                t = getattr(P, m)(*a, **k)
                if r is not None:
                    r[0] = t


class Ctx:
    N = [0]

    def __init__(self, nc):
        self.nc = nc
        self.es = ExitStack()

    def sb(self, shape, dt, name=None):
        Ctx.N[0] += 1
        return self.es.enter_context(self.nc.sbuf_tensor("%s_%d" % (name or "t", Ctx.N[0]), list(shape), dt))

    def ps(self, shape, dt, name=None):
        Ctx.N[0] += 1
        return self.es.enter_context(self.nc.psum_tensor("%s_%d" % (name or "p", Ctx.N[0]), list(shape), dt))

    def __enter__(self):
        return self

    def __exit__(self, *a):
        return self.es.__exit__(*a)


def load_gbc(P, c, g_row):
    gbc = c.sb([128, D], F32, "gbc")
    b = Buf()
    P.dma("sp", gbc[:], g_row.partition_broadcast(128), writes=[b])
    return gbc, b


def load_gidx(P, c, gidx_d):
    gix = c.sb([128, 80], mybir.dt.int32, "gix")
    b = Buf()
    P.dma("sp", gix[:], gidx_d, writes=[b])
    return gix, b


def load_ident(P, c, ident_d):
    ident = c.sb([128, 128], BF16, "ident")
    b = Buf()
    P.dma("sp", ident[:], ident_d, writes=[b])
    return ident, b


def norm_transpose(P, c, x_src, row0, ntile, gbc, gbc_b, ident, ident_b, hT, hT_b, psT, psT_b, st):
    xt2, xt_b2, hb2, hb_b2, ss, ss_b, junk, junk_b = st

    def stage_a(tt):
        r0 = row0 + tt * 128
        xt = xt2[:, tt % 2]
        xt_b = xt_b2[tt % 2]
        hb = hb2[:, tt % 2]
        hb_b = hb_b2[tt % 2]
        sc = ss[:, tt % 2]
        sc_b = ss_b[tt % 2]
        P.dma("sp", xt, x_src[r0:r0 + 128, :], writes=[xt_b])
        P.op("act", "activation", out=junk[:], in_=xt, func=AF.Square, accum_out=sc[:, 0:1],
             reads=[xt_b], writes=[junk_b, sc_b])
        P.op("act", "activation", out=sc[:, 1:2], in_=sc[:, 0:1], func=AF.Sqrt, scale=1.0 / D, bias=sc[:, 3:4],
             reads=[sc_b], writes=[sc_b])
        P.op("dve", "reciprocal", out=sc[:, 2:3], in_=sc[:, 1:2], reads=[sc_b], writes=[sc_b])
        P.op("dve", "scalar_tensor_tensor", out=hb, in0=xt, scalar=sc[:, 2:3], in1=gbc[:],
             op0=ALU.mult, op1=ALU.mult, reads=[xt_b, sc_b, gbc_b], writes=[hb_b])

    def stage_b(tt):
        hb = hb2[:, tt % 2]
        hb_b = hb_b2[tt % 2]
        for g4 in range(4):
            pb = g4 % 2
            for j in range(4):
                kc = g4 * 4 + j
                P.op("pe", "transpose", out=psT[pb][:, j * 128:(j + 1) * 128], in_=hb[:, kc * 128:(kc + 1) * 128],
                     identity=ident[:], reads=[hb_b, ident_b], writes=[psT_b[pb]])
            src = psT[pb][:, 0:512].rearrange("p (a b) -> p a b", a=4)
            dst = hT[:, g4 * 4:(g4 + 1) * 4, tt * 128:(tt + 1) * 128]
            if g4 % 2 == 0:
                P.op("act", "activation", out=dst, in_=src, func=AF.Copy, reads=[psT_b[pb]], writes=[hT_b[tt]])
            else:
                P.op("dve", "tensor_copy", out=dst, in_=src, reads=[psT_b[pb]], writes=[hT_b[tt]])

    stage_a(0)
    for tt in range(ntile):
        if tt + 1 < ntile:
            stage_a(tt + 1)
        stage_b(tt)


def norm_state(P, c):
    xt = c.sb([128, 2, D], F32, "xt")
    hb = c.sb([128, 2, D], BF16, "hb")
    junk = c.sb([128, D], BF16, "junk")
    ss = c.sb([128, 2, 4], F32, "ss")
    ss_b = bufs(2)
    for b in range(2):
        P.op("dve", "memset", ap=ss[:, b, 3:4], constant=EPS, writes=[ss_b[b]])
    return (xt, bufs(2), hb, bufs(2), ss, ss_b, junk, Buf())


def ffn_phase(P, x_src, x_dst, g_row, Wg, Wu, Wd, ident_d):
    nc = P.nc
    TP = 1024
    NT = TP // 128
    with Ctx(nc) as c:
        hT = c.sb([128, 16, TP], BF16, "hT")
        aT = c.sb([128, 44, TP], BF16, "aT")
        wg = c.sb([128, 2, 16, 256], BF16, "wg")
        wu = c.sb([128, 2, 16, 256], BF16, "wu")
        wd = c.sb([128, 3, 4, 512], BF16, "wd")
        sg = c.sb([128, 2, 512], BF16, "sg")
        ps = [c.ps([128, 512], F32, "ps") for _ in range(8)]
        ps_b = bufs(8)
        psT = [ps[4][:].bitcast(BF16), ps[5][:].bitcast(BF16)]
        psT_b = [ps_b[4], ps_b[5]]
        gbc, gbc_b = load_gbc(P, c, g_row)
        ident, ident_b = load_ident(P, c, ident_d)
        st = norm_state(P, c)
        hT_b = bufs(NT)
        aT_b = [[Buf() for _ in range(2)] for _ in range(44)]
        wg_b, wu_b, wd_b, sg_b, xf_b = bufs(2), bufs(2), bufs(3), bufs(2), bufs(8)
        for p in range(TOK // TP):
            row0 = p * TP
            norm_transpose(P, c, x_src, row0, NT, gbc, gbc_b, ident, ident_b, hT, hT_b, psT, psT_b, st)
            for fg in range(22):
                b = fg % 2
                fs = slice(fg * 256, (fg + 1) * 256)
                P.dma("pool", wg[:, b], Wg[:, fs].rearrange("(kc p) f -> p kc f", p=128), writes=[wg_b[b]])
                P.dma("pool", wu[:, b], Wu[:, fs].rearrange("(kc p) f -> p kc f", p=128), writes=[wu_b[b]])
                for fc in range(2):
                    f = fg * 2 + fc
                    for th in range(2):
                        i2 = (f * 2 + th) % 2
                        G, Gb = ps[i2], ps_b[i2]
                        U, Ub = ps[2 + i2], ps_b[2 + i2]
                        hr = hT_b[th * 4:(th + 1) * 4]
                        for kc in range(16):
                            P.op("pe", "matmul", out=G[:], lhsT=wg[:, b, kc, fc * 128:(fc + 1) * 128],
                                 rhs=hT[:, kc, th * 512:(th + 1) * 512], start=(kc == 0), stop=(kc == 15),
                                 reads=[wg_b[b]] + hr, writes=[Gb])
                        for kc in range(16):
                            P.op("pe", "matmul", out=U[:], lhsT=wu[:, b, kc, fc * 128:(fc + 1) * 128],
                                 rhs=hT[:, kc, th * 512:(th + 1) * 512], start=(kc == 0), stop=(kc == 15),
                                 reads=[wu_b[b]] + hr, writes=[Ub])
                        P.op("act", "activation", out=sg[:, i2], in_=G[:], func=AF.Silu, reads=[Gb], writes=[sg_b[i2]])
                        P.op("dve", "tensor_tensor", out=aT[:, f, th * 512:(th + 1) * 512], in0=sg[:, i2], in1=U[:],
                             op=ALU.mult, reads=[sg_b[i2], Ub], writes=[aT_b[f][th]])
            nwd = 0
            xt2, xt_b2 = st[0], st[1]
            for dc in range(4):
                ds_ = slice(dc * 512, (dc + 1) * 512)
                for tt in range(NT):
                    r0 = row0 + tt * 128
                    xblk = xt2[:, tt // 4, (tt % 4) * 512:(tt % 4 + 1) * 512]
                    if dc == 0 and tt % 4 == 0:
                        P.dma("sp", xblk, x_src[r0:r0 + 128, ds_], writes=[xf_b[tt], xt_b2[tt // 4]])
                    else:
                        P.dma("sp", xblk, x_src[r0:r0 + 128, ds_], reads=[xt_b2[tt // 4]], writes=[xf_b[tt]])
                for fq in range(11):
                    wb = nwd % 3
                    nwd += 1
                    P.dma("pool", wd[:, wb], Wd[fq * 512:(fq + 1) * 512, ds_].rearrange("(fc p) d -> p fc d", p=128),
                          writes=[wd_b[wb]])
                    for tt in range(NT):
                        for fl in range(4):
                            f = fq * 4 + fl
                            P.op("pe", "matmul", out=ps[tt][:], lhsT=aT[:, f, tt * 128:(tt + 1) * 128],
                                 rhs=wd[:, wb, fl, :], start=(f == 0), stop=(f == 43),
                                 reads=[wd_b[wb], aT_b[f][tt // 4]], writes=[ps_b[tt]])
                for tt in range(NT):
                    xblk = xt2[:, tt // 4, (tt % 4) * 512:(tt % 4 + 1) * 512]
                    P.op("dve", "scalar_tensor_tensor", out=xblk, in0=ps[tt][:], scalar=0.5, in1=xblk,
                         op0=ALU.mult, op1=ALU.add, reads=[ps_b[tt]], writes=[xf_b[tt]])
                for tt in range(NT):
                    r0 = row0 + tt * 128
                    xblk = xt2[:, tt // 4, (tt % 4) * 512:(tt % 4 + 1) * 512]
                    P.dma("sp", x_dst[r0:r0 + 128, ds_], xblk, reads=[xf_b[tt], xt_b2[tt // 4]])
        P.barrier()
        P.emit()


def win_phase(P, x_src, g_row, Win, ident_d, zqk, zv, zr, zqk_g, zv_g, zr_g):
    nc = P.nc
    NT = TOK // 128
    with Ctx(nc) as c:
        hT = c.sb([128, 16, TOK], BF16, "hT")
        wt = c.sb([128, 3, 16, 256], BF16, "wt")
        wv = c.sb([128, 16, 1024], BF16, "wv")
        stf = c.sb([128, 3, TOK], F32, "stf")
        stb = c.sb([128, 3, TOK], BF16, "stb")
        vst = c.sb([128, 2, 1024], BF16, "vst")
        ps = [c.ps([128, 512], F32, "ps") for _ in range(8)]
        ps_b = bufs(8)
        psT = [ps[4][:].bitcast(BF16), ps[5][:].bitcast(BF16)]
        psT_b = [ps_b[4], ps_b[5]]
        gbc, gbc_b = load_gbc(P, c, g_row)
        ident, ident_b = load_ident(P, c, ident_d)
        st = norm_state(P, c)
        hT_b = bufs(NT)
        wt_b, stf_b, stb_b, vst_b = bufs(3), bufs(3), bufs(3), bufs(2)
        wv_b = Buf()
        for q4 in range(4):
            P.dma("pool", wv[:, :, q4 * 256:(q4 + 1) * 256],
                  Win[:, 2048 + q4 * 256:2048 + (q4 + 1) * 256].rearrange("(kc p) f -> p kc f", p=128), writes=[wv_b])
        norm_transpose(P, c, x_src, 0, NT, gbc, gbc_b, ident, ident_b, hT, hT_b, psT, psT_b, st)
        groups = []
        for j in range(4):
            groups.append((256 * j, [("qk", j, 0), ("qk", j, 1)]))
        for j in range(4):
            groups.append((1024 + 256 * j, [("qk", j, 2), ("qk", j, 3)]))
        groups.append("V")
        for m in range(2):
            groups.append((3072 + 256 * m, [("r", 2 * m, 0), ("r", 2 * m + 1, 0)]))
        for m in range(2):
            groups.append((3584 + 256 * m, [("r", 2 * m, 1), ("r", 2 * m + 1, 1)]))
        for m in range(2):
            groups.append((4096 + 256 * m, [("r", 2 * m, 2), ("r", 2 * m + 1, 2)]))
        cc0 = getattr(P, "ncc", 0)
        def do_v():
            vtoks = []
            for tt in range(NT):
                vb = tt % 2
                for half in range(2):
                    pi = 6 + (tt * 2 + half) % 2
                    for kc in range(16):
                        P.op("pe", "matmul", out=ps[pi][:], lhsT=hT[:, kc, tt * 128:(tt + 1) * 128],
                             rhs=wv[:, kc, half * 512:(half + 1) * 512], start=(kc == 0), stop=(kc == 15),
                             reads=[wv_b, hT_b[tt]], writes=[ps_b[pi]])
                    dst = vst[:, vb, half * 512:(half + 1) * 512]
                    if half == 0:
                        P.op("act", "activation", out=dst, in_=ps[pi][:], func=AF.Copy, reads=[ps_b[pi]],
                             writes=[vst_b[vb]])
                    else:
                        P.op("dve", "tensor_copy", out=dst, in_=ps[pi][:], reads=[ps_b[pi]], writes=[vst_b[vb]])
                vtoks.append(P.dma("sp", zv[:, :, tt, :].rearrange("j p c -> p j c"),
                                   vst[:, vb].rearrange("p (j c) -> p j c", j=4), reads=[vst_b[vb]]))
            for j in range(4):
                P.allgather(zv[j].rearrange("p b c -> p (b c)"), zv_g[j * 512:(j + 1) * 512, :], vtoks)

        fgroups = [g for g in groups if g != "V"]

        def load_w(k):
            if k < len(fgroups):
                c0 = fgroups[k][0]
                P.dma("pool", wt[:, k % 3], Win[:, c0:c0 + 256].rearrange("(kc p) f -> p kc f", p=128),
                      writes=[wt_b[k % 3]])

        load_w(0)
        load_w(1)
        nps = 0
        nst = 0
        gi = 0
        for grp in groups:
            if grp == "V":
                do_v()
                continue
            col0, chunks = grp
            b = gi % 3
            gi += 1
            load_w(gi + 1)
            toks = []
            for ci, (kind, dest, idx) in enumerate(chunks):
                sb_i = nst % 3
                nst += 1
                if kind == "qk":
                    stg, stg_b = stb, stb_b
                else:
                    stg, stg_b = stf, stf_b
                for th in range(4):
                    pi = nps % 4
                    nps += 1
                    for kc in range(16):
                        P.op("pe", "matmul", out=ps[pi][:], lhsT=wt[:, b, kc, ci * 128:(ci + 1) * 128],
                             rhs=hT[:, kc, th * 512:(th + 1) * 512], start=(kc == 0), stop=(kc == 15),
                             reads=[wt_b[b]] + hT_b[th * 4:(th + 1) * 4], writes=[ps_b[pi]])
                    dst = stg[:, sb_i, th * 512:(th + 1) * 512]
                    if th % 2 == 0:
                        P.op("act", "activation", out=dst, in_=ps[pi][:], func=AF.Copy, reads=[ps_b[pi]],
                             writes=[stg_b[sb_i]])
                    else:
                        P.op("dve", "tensor_copy", out=dst, in_=ps[pi][:], reads=[ps_b[pi]], writes=[stg_b[sb_i]])
                if kind == "qk":
                    toks.append(P.dma("sp", zqk[dest, idx], stg[:, sb_i], reads=[stg_b[sb_i]]))
                else:
                    t = P.dma("sp", zr[dest, idx], stg[:, sb_i], reads=[stg_b[sb_i]])
                    k = dest * 3 + idx
                    P.allgather(zr[dest, idx], zr_g[k * 512:(k + 1) * 512, :], [t])
            if chunks[0][0] == "qk":
                j, cp = chunks[0][1], chunks[0][2] // 2
                k = j * 2 + cp
                P.allgather(zqk[j, 2 * cp:2 * cp + 2].rearrange("c p t -> (c p) t"), zqk_g[k * 1024:(k + 1) * 1024, :], toks)
        P.barrier(cc=cc0 + 12)
        P.emit()
    return cc0 + 24


WOUT_ROW = lambda j, cc: (2 * j + cc) * 128 if cc < 2 else (1024 + j * 128 if cc == 2 else 1536 + j * 128)


def wout_phase(P, x_src, x_dst, yg, gidx_d, Wout, cc_all):
    nc = P.nc
    NT = TOK // 128
    with Ctx(nc) as c:
        yT = c.sb([128, 16, TOK], BF16, "yT")
        wo = c.sb([128, 16, D], BF16, "wo")
        xr = c.sb([128, 4, 512], F32, "xr")
        ps = [c.ps([128, 512], F32, "ps") for _ in range(8)]
        ps_b = bufs(8)
        yT_b, wo_b, xr_b = bufs(4), bufs(16), bufs(4)
        gix, gix_b = load_gidx(P, c, gidx_d)
        ydep = Buf()
        ydep.w = ("cc", cc_all)
        for j in range(4):
            for cc in range(4):
                r0 = WOUT_ROW(j, cc)
                P.dma("pool", wo[:, j * 4 + cc, :], Wout[r0:r0 + 128, :], writes=[wo_b[j * 4 + cc]])
        for j in range(4):
            for cc in range(4):
                P.idma(yT[:, j * 4 + cc, :], yg, gix[:, 32 + j * 4 + cc:33 + j * 4 + cc], reads=[gix_b, ydep],
                       writes=[yT_b[j]])
        tiles = [(tt, dc) for tt in range(NT) for dc in range(4)]

        def load_x(n):
            if n < len(tiles):
                tt, dc = tiles[n]
                P.dma("sp", xr[:, n % 4], x_src[tt * 128:(tt + 1) * 128, dc * 512:(dc + 1) * 512], writes=[xr_b[n % 4]])

        load_x(0)
        load_x(1)
        for n, (tt, dc) in enumerate(tiles):
            pi = n % 8
            xb = n % 4
            ds_ = slice(dc * 512, (dc + 1) * 512)
            for ch in range(16):
                P.op("pe", "matmul", out=ps[pi][:], lhsT=yT[:, ch, tt * 128:(tt + 1) * 128], rhs=wo[:, ch, ds_],
                     start=(ch == 0), stop=(ch == 15), reads=[yT_b[ch // 4], wo_b[ch]], writes=[ps_b[pi]])
            load_x(n + 2)
            r0 = tt * 128
            P.op("dve", "tensor_tensor", out=xr[:, xb], in0=ps[pi][:], in1=xr[:, xb], op=ALU.add,
                 reads=[ps_b[pi]], writes=[xr_b[xb]])
            P.dma("sp", x_dst[r0:r0 + 128, ds_], xr[:, xb], reads=[xr_b[xb]])
        P.barrier()
        P.emit()


def fnorm_phase(P, x_src, out, g_row):
    nc = P.nc
    with Ctx(nc) as c:
        gbc, gbc_b = load_gbc(P, c, g_row)
        xt = c.sb([128, 2, D], F32, "xt")
        ot = c.sb([128, 2, D], F32, "ot")
        ss = c.sb([128, 2, 4], F32, "ss")
        xt_b, ot_b, ss_b = bufs(2), bufs(2), bufs(2)
        for b in range(2):
            P.op("dve", "memset", ap=ss[:, b, 3:4], constant=EPS, writes=[ss_b[b]])
        for tt in range(TOK // 128):
            b = tt % 2
            r0 = tt * 128
            P.dma("sp", xt[:, b], x_src[r0:r0 + 128, :], writes=[xt_b[b]])
            P.op("act", "activation", out=ot[:, b], in_=xt[:, b], func=AF.Square, accum_out=ss[:, b, 0:1],
                 reads=[xt_b[b]], writes=[ot_b[b], ss_b[b]])
            P.op("act", "activation", out=ss[:, b, 1:2], in_=ss[:, b, 0:1], func=AF.Sqrt, scale=1.0 / D,
                 bias=ss[:, b, 3:4], reads=[ss_b[b]], writes=[ss_b[b]])
            P.op("dve", "reciprocal", out=ss[:, b, 2:3], in_=ss[:, b, 1:2], reads=[ss_b[b]], writes=[ss_b[b]])
            P.op("dve", "scalar_tensor_tensor", out=ot[:, b], in0=xt[:, b], scalar=ss[:, b, 2:3], in1=gbc[:],
                 op0=ALU.mult, op1=ALU.mult, reads=[xt_b[b], ss_b[b], gbc_b], writes=[ot_b[b]])
            P.dma("sp", out[r0:r0 + 128, :], ot[:, b], reads=[ot_b[b]])
        P.barrier()
        P.emit()


def rnnpool_body(P, c, zr_g, gix, gix_b, pvec_d, mats_d, invcnt_d, yout, yg, zdep):
    CW = 1024
    W = 16 + CW
    zr_h = zr_g.rearrange("r (h t) -> (r h) t", h=2)
    if True:
        pv = c.sb([128, 16], F32, "pv")
        mats = c.sb([128, 3, 128], BF16, "mats")
        sm = c.sb([128, 8], F32, "sm")
        xe = c.sb([128, 3 + CW], F32, "xe")
        gt = c.sb([128, CW], F32, "gt")
        u = c.sb([128, CW], F32, "u")
        ub = c.sb([128, CW], BF16, "ub")
        ra = c.sb([128, CW], F32, "ra")
        ii = c.sb([128, CW], F32, "ii")
        t1 = c.sb([128, CW], F32, "t1")
        h = c.sb([128, CW], F32, "h")
        ob = c.sb([128, 2, CW], BF16, "ob")
        pe_ = c.sb([128, W], F32, "pe")
        y2 = c.sb([128, W], F32, "y2")
        y4 = c.sb([128, W], F32, "y4")
        y8 = c.sb([128, W], F32, "y8")
        y16 = c.sb([128, W], F32, "y16")
        ic = c.sb([128, CW], F32, "ic")
        sl = c.sb([128, CW], F32, "sl")
        db = c.sb([128, CW], BF16, "db")
        ps = [c.ps([128, 512], F32, "ps") for _ in range(2)]
        ps_b = bufs(2)
        (pv_b, mats_b, sm_b, xe_b, gt_b, u_b, ub_b, ra_b, ii_b, t1_b, h_b, pe_b, y2_b, y4_b, y8_b, y16_b,
         ic_b, sl_b, db_b) = bufs(19)
        ob_b = bufs(2)
        P.dma("sp", pv[:], pvec_d, writes=[pv_b])
        P.dma("pool", mats[:], mats_d.rearrange("m p q -> p m q"), writes=[mats_b])
        P.op("dve", "memset", ap=sm[:, 3:4], constant=1.0, writes=[sm_b])
        P.op("act", "activation", out=sm[:, 0:1], in_=pv[:, 7:8], func=AF.Exp, scale=-1.0, reads=[pv_b, sm_b], writes=[sm_b])
        P.op("act", "activation", out=sm[:, 1:2], in_=sm[:, 0:1], func=AF.Ln, bias=sm[:, 3:4], reads=[sm_b], writes=[sm_b])
        P.op("dve", "tensor_scalar", out=sm[:, 2:3], in0=sm[:, 1:2], scalar1=-8.0, scalar2=None, op0=ALU.mult,
             reads=[sm_b], writes=[sm_b])
        npi = 0
        ytoks = []
        for ci in range(8):
            i = ci
            src_i, hh = ci // 2, ci % 2
            if i == 0:
                P.op("dve", "memset", ap=xe[:, 0:3], constant=0.0, writes=[xe_b])
                P.op("dve", "memset", ap=pe_[:, 0:16], constant=0.0, writes=[pe_b])
            else:
                P.op("dve", "tensor_copy", out=xe[:, 0:3], in_=xe[:, CW:CW + 3], reads=[xe_b], writes=[xe_b])
                P.op("dve", "tensor_copy", out=pe_[:, 0:16], in_=pe_[:, CW:CW + 16], reads=[pe_b], writes=[pe_b])
            def gcol(cc_):
                k_ = 48 + (src_i * 3 + cc_) * 2 + hh
                return gix[:, k_:k_ + 1]
            P.idma(xe[:, 3:3 + CW], zr_h, gcol(1), reads=[gix_b, zdep], writes=[xe_b])
            P.idma(gt[:], zr_h, gcol(0), reads=[gix_b, zdep], writes=[gt_b])
            P.idma(pe_[:, 16:W], zr_h, gcol(2), reads=[gix_b, zdep], writes=[pe_b])
            P.dma("sp", ic[:], invcnt_d[:, i * CW:(i + 1) * CW], writes=[ic_b])
            P.op("dve", "tensor_scalar", out=u[:], in0=xe[:, 0:CW], scalar1=pv[:, 0:1], scalar2=pv[:, 4:5],
                 op0=ALU.mult, op1=ALU.add, reads=[xe_b, pv_b], writes=[u_b])
            for k in range(1, 4):
                P.op("dve", "scalar_tensor_tensor", out=u[:], in0=xe[:, k:k + CW], scalar=pv[:, k:k + 1], in1=u[:],
                     op0=ALU.mult, op1=ALU.add, reads=[xe_b, pv_b], writes=[u_b])
            P.op("pool", "tensor_copy", out=ub[:], in_=u[:], reads=[u_b], writes=[ub_b])
            for q in range(CW // 512):
                qs = slice(q * 512, (q + 1) * 512)
                pa = npi % 2
                pb = (npi + 1) % 2
                npi += 2
                P.op("pe", "matmul", out=ps[pa][:], lhsT=mats[:, 0, :], rhs=ub[:, qs], start=True, stop=True,
                     reads=[mats_b, ub_b], writes=[ps_b[pa]])
                P.op("pe", "matmul", out=ps[pb][:], lhsT=mats[:, 1, :], rhs=ub[:, qs], start=True, stop=True,
                     reads=[mats_b, ub_b], writes=[ps_b[pb]])
                P.op("act", "activation", out=ra[:, qs], in_=ps[pa][:], func=AF.Sigmoid, bias=pv[:, 5:6],
                     reads=[ps_b[pa], pv_b], writes=[ra_b])
                P.op("act", "activation", out=ii[:, qs], in_=ps[pb][:], func=AF.Sigmoid, bias=pv[:, 6:7],
                     reads=[ps_b[pb], pv_b], writes=[ii_b])
            P.op("act", "activation", out=ra[:], in_=ra[:], func=AF.Exp, scale=sm[:, 2:3], reads=[sm_b], writes=[ra_b])
            P.op("pool", "tensor_tensor", out=t1[:], in0=ra[:], in1=ra[:], op=ALU.mult, reads=[ra_b], writes=[t1_b])
            P.op("dve", "tensor_scalar", out=t1[:], in0=t1[:], scalar1=-1.0, scalar2=1.0, op0=ALU.mult, op1=ALU.add,
                 reads=[], writes=[t1_b])
            P.op("act", "activation", out=t1[:], in_=t1[:], func=AF.Sqrt, reads=[], writes=[t1_b])
            P.op("dve", "tensor_tensor", out=ii[:], in0=ii[:], in1=u[:], op=ALU.mult, reads=[u_b], writes=[ii_b])
            P.op("dve", "tensor_tensor", out=ii[:], in0=ii[:], in1=t1[:], op=ALU.mult, reads=[t1_b], writes=[ii_b])
            if i > 0:
                P.op("dve", "tensor_copy", out=sm[:, 4:5], in_=h[:, CW - 1:CW], reads=[h_b], writes=[sm_b])
            P.op("dve", "tensor_tensor_scan", out=h[:], data0=ra[:], data1=ii[:],
                 initial=(0.0 if i == 0 else sm[:, 4:5]), op0=ALU.mult, op1=ALU.add,
                 reads=[ra_b, ii_b, sm_b], writes=[h_b])
            P.op("pool", "tensor_tensor", out=t1[:], in0=gt[:], in1=gt[:], op=ALU.mult, reads=[gt_b], writes=[t1_b])
            P.op("dve", "tensor_scalar", out=t1[:], in0=t1[:], scalar1=0.044715, scalar2=1.0, op0=ALU.mult, op1=ALU.add,
                 reads=[], writes=[t1_b])
            P.op("pool", "tensor_tensor", out=t1[:], in0=t1[:], in1=gt[:], op=ALU.mult, reads=[gt_b], writes=[t1_b])
            P.op("act", "activation", out=t1[:], in_=t1[:], func=AF.Sigmoid, scale=1.5957691216057308, reads=[],
                 writes=[t1_b])
            P.op("dve", "tensor_tensor", out=t1[:], in0=t1[:], in1=gt[:], op=ALU.mult, reads=[gt_b], writes=[t1_b])
            P.op("dve", "tensor_tensor", out=ob[:, 0], in0=t1[:], in1=h[:], op=ALU.mult, reads=[t1_b, h_b],
                 writes=[ob_b[0]])
            ytoks.append(P.dma("sp", yout[src_i, 2, :, hh * CW:(hh + 1) * CW], ob[:, 0], reads=[ob_b[0]]))
            P.op("pool", "tensor_tensor", out=y2[:, 1:W], in0=pe_[:, 1:W], in1=pe_[:, 0:W - 1], op=ALU.add,
                 reads=[pe_b], writes=[y2_b])
            P.op("pool", "tensor_tensor", out=y4[:, 3:W], in0=y2[:, 3:W], in1=y2[:, 1:W - 2], op=ALU.add,
                 reads=[y2_b], writes=[y4_b])
            P.op("pool", "tensor_tensor", out=y8[:, 7:W], in0=y4[:, 7:W], in1=y4[:, 3:W - 4], op=ALU.add,
                 reads=[y4_b], writes=[y8_b])
            P.op("pool", "tensor_tensor", out=y16[:, 15:W], in0=y8[:, 15:W], in1=y8[:, 7:W - 8], op=ALU.add,
                 reads=[y8_b], writes=[y16_b])
            P.op("dve", "tensor_scalar", out=sl[:], in0=y2[:, 16:W], scalar1=pv[:, 9:10], scalar2=None, op0=ALU.mult,
                 reads=[y2_b, pv_b], writes=[sl_b])
            for k, (yy, yy_b) in enumerate(((y4, y4_b), (y8, y8_b), (y16, y16_b))):
                P.op("dve", "scalar_tensor_tensor", out=sl[:], in0=yy[:, 16:W], scalar=pv[:, 10 + k:11 + k], in1=sl[:],
                     op0=ALU.mult, op1=ALU.add, reads=[yy_b, pv_b], writes=[sl_b])
            P.op("pool", "tensor_tensor", out=sl[:], in0=sl[:], in1=ic[:], op=ALU.mult, reads=[ic_b], writes=[sl_b])
            P.op("dve", "tensor_tensor", out=db[:], in0=sl[:], in1=pe_[:, 16:W], op=ALU.subtract, reads=[sl_b, pe_b],
                 writes=[db_b])
            for q in range(CW // 512):
                qs = slice(q * 512, (q + 1) * 512)
                pa = npi % 2
                npi += 1
                P.op("pe", "matmul", out=ps[pa][:], lhsT=mats[:, 2, :], rhs=db[:, qs], start=True, stop=True,
                     reads=[mats_b, db_b], writes=[ps_b[pa]])
                P.op("dve", "tensor_scalar", out=ob[:, 1, qs], in0=ps[pa][:], scalar1=pv[:, 8:9], scalar2=None,
                     op0=ALU.mult, reads=[ps_b[pa], pv_b], writes=[ob_b[1]])
            ytoks.append(P.dma("sp", yout[src_i, 3, :, hh * CW:(hh + 1) * CW], ob[:, 1], reads=[ob_b[1]]))
            if hh == 1:
                P.allgather(yout[src_i, 2:4].rearrange("c p t -> (c p) t"),
                            yg[(src_i * 2 + 1) * 1024:(src_i * 2 + 2) * 1024, :], ytoks[-4:])


def attn_phase(P, zqk_g, zv_g, zr_g, gidx_d, masks_d, tri_d, pvec_d, mats_d, invcnt_d, yout, yg, cc_all):
    nc = P.nc
    SCALE = 128.0 ** -0.5
    NQ = S // 512
    with Ctx(nc) as c:
        qT = c.sb([128, 2, S], BF16, "qT")
        kT = c.sb([128, 2, S], BF16, "kT")
        vv = c.sb([128, 4, 16, 256], BF16, "vv")
        mk = c.sb([128, 4, 512], F32, "mk")
        tc_ = c.sb([128, 2, 128], BF16, "tc")
        one = c.sb([128, 1], F32, "one")
        eb = c.sb([128, 2, 3, 512], F32, "eb")
        spb = c.sb([128, 2, 2, 512], BF16, "spb")
        xb = c.sb([128, 2, 2, 512], F32, "xb")
        ab = c.sb([128, 2, 2, 512], BF16, "ab")
        ys = c.sb([128, 2, 2, 512], BF16, "ys")
        psS = [c.ps([128, 512], F32, "psS") for _ in range(2)]
        psT = [c.ps([128, 512], F32, "psT") for _ in range(2)]
        psO = [[c.ps([128, 512], F32, "psO")] * 2 for _ in range(2)]
        psS_b, psT_b = bufs(2), bufs(2)
        psO_b = [[Buf()] * 2, [Buf()] * 2]
        q_b, k_b, v_b = [bufs(4), bufs(4)], [bufs(4), bufs(4)], bufs(4)
        mk_b, tc_b, one_b = Buf(), Buf(), Buf()
        eb_b = [bufs(3), bufs(3)]
        sp_b = [bufs(2), bufs(2)]
        x_b = [bufs(2), bufs(2)]
        a_b = [bufs(2), bufs(2)]
        ys_b = [bufs(2), bufs(2)]
        P.dma("sp", mk[:], masks_d, writes=[mk_b])
        P.dma("sp", tc_[:], tri_d, writes=[tc_b])
        P.op("dve", "memset", ap=one[:], constant=1.0, writes=[one_b])
        zt = c.sb([128, 512], BF16, "zt")
        zt_b = Buf()
        P.op("dve", "memset", ap=zt[:], constant=0.0, writes=[zt_b])
        gix, gix_b = load_gidx(P, c, gidx_d)
        for i in range(4):
            first = [] if i == 0 else [q_b[0][0], q_b[1][0], k_b[0][0], k_b[1][0], v_b[0]]
            for hd in range(2):
                P.idma(qT[:, hd, i * TOK:(i + 1) * TOK], zqk_g, gix[:, i * 4 + hd:i * 4 + hd + 1],
                       reads=[gix_b] + first, writes=[q_b[hd][i]])
                P.idma(kT[:, hd, i * TOK:(i + 1) * TOK], zqk_g, gix[:, i * 4 + 2 + hd:i * 4 + 3 + hd],
                       reads=[gix_b] + first, writes=[k_b[hd][i]])
            P.idma(vv[:, i].rearrange("p b c -> p (b c)"), zv_g, gix[:, 28 + i:29 + i], reads=[gix_b] + first,
                   writes=[v_b[i]])
        zdep = Buf()
        zdep.w = ("cc", cc_all)
        rec = Rec()
        rnnpool_body(rec, c, zr_g, gix, gix_b, pvec_d, mats_d, invcnt_d, yout, yg, zdep)
        atoks = {}
        blocks = []
        for qi in range(NQ):
            for kb in range(4 * qi + 3, -1, -1):
                blocks.append((qi, kb))
        NBLK = len(blocks)

        def lo_of(n):
            qi, kb = blocks[n]
            return max(kb - 4 * qi, 0) * 128

        def stage1(cn, n):
            qi, kb = blocks[n]
            e_i = n % 3
            s_i = n % 2
            lo = lo_of(n)
            P.op("pe", "matmul", out=psS[cn][:, lo:512], lhsT=kT[:, cn, kb * 128:(kb + 1) * 128],
                 rhs=qT[:, cn, qi * 512 + lo:(qi + 1) * 512], start=True, stop=True,
                 reads=[k_b[cn][kb // 16], q_b[cn][qi // 4]],
                 writes=[psS_b[cn]])
            P.op("act", "activation", out=eb[:, cn, e_i, lo:512], in_=psS[cn][:, lo:512], func=AF.Exp, scale=SCALE,
                 reads=[psS_b[cn]], writes=[eb_b[cn][e_i]])
            r = kb - 4 * qi
            if r >= 0:
                P.op("dve", "tensor_tensor", out=eb[:, cn, e_i, lo:512], in0=eb[:, cn, e_i, lo:512], in1=mk[:, r, lo:512],
                     op=ALU.mult, reads=[mk_b], writes=[eb_b[cn][e_i]])
            P.op("act", "activation", out=spb[:, cn, s_i, lo:512], in_=eb[:, cn, e_i, lo:512], func=AF.Ln, bias=1.0,
                 reads=[eb_b[cn][e_i]], writes=[sp_b[cn][s_i]])

        def pe_tri(cn, n):
            qi, kb = blocks[n]
            first = (kb == 4 * qi + 3)
            lo = lo_of(n)
            if first:
                P.op("pe", "matmul", out=psT[cn][:], lhsT=tc_[:, 0, :], rhs=zt[:], start=True, stop=False,
                     skip_group_check=True, reads=[tc_b, zt_b], writes=[psT_b[cn]])
            P.op("pe", "matmul", out=psT[cn][:, lo:512], lhsT=tc_[:, 0, :], rhs=spb[:, cn, n % 2, lo:512], start=False,
                 stop=False, skip_group_check=True, reads=[tc_b, sp_b[cn][n % 2]], writes=[psT_b[cn]])

        def act_x(cn, n):
            lo = lo_of(n)
            P.op("act", "activation", out=xb[:, cn, n % 2, lo:512], in_=psT[cn][:, lo:512], func=AF.Exp, scale=-1.0,
                 reads=[psT_b[cn]], writes=[x_b[cn][n % 2]])

        def pe_comp(cn, n):
            qi, kb = blocks[n]
            if kb == 0:
                return
            lo = lo_of(n)
            P.op("pe", "matmul", out=psT[cn][:, lo:512], lhsT=tc_[:, 1, :], rhs=spb[:, cn, n % 2, lo:512], start=False,
                 stop=False, skip_group_check=True, reads=[tc_b, sp_b[cn][n % 2]], writes=[psT_b[cn]])

        def dve_a(cn, n):
            lo = lo_of(n)
            P.op("dve", "tensor_tensor", out=ab[:, cn, n % 2, lo:512], in0=eb[:, cn, n % 3, lo:512],
                 in1=xb[:, cn, n % 2, lo:512], op=ALU.mult,
                 reads=[eb_b[cn][n % 3], x_b[cn][n % 2]], writes=[a_b[cn][n % 2]])

        def pe_av(cn, n):
            qi, kb = blocks[n]
            first = (kb == 4 * qi + 3)
            o_i = qi % 2
            lo = lo_of(n)
            if first:
                P.op("pe", "matmul", out=psO[cn][o_i][:], lhsT=tc_[:, 0, :], rhs=zt[:], start=True, stop=False,
                     skip_group_check=True, reads=[tc_b, zt_b], writes=[psO_b[cn][o_i]])
            P.op("pe", "matmul", out=psO[cn][o_i][:, lo:512], lhsT=vv[:, kb // 16, kb % 16, cn * 128:(cn + 1) * 128],
                 rhs=ab[:, cn, n % 2, lo:512], start=False, skip_group_check=True,
                 stop=(kb == 0), reads=[v_b[kb // 16], a_b[cn][n % 2]], writes=[psO_b[cn][o_i]])
            if kb == 0:
                y_i = qi % 2
                if cn == 0:
                    P.op("act", "activation", out=ys[:, cn, y_i], in_=psO[cn][o_i][:], func=AF.Copy,
                         reads=[psO_b[cn][o_i]], writes=[ys_b[cn][y_i]])
                else:
                    P.op("dve", "tensor_copy", out=ys[:, cn, y_i], in_=psO[cn][o_i][:], reads=[psO_b[cn][o_i]],
                         writes=[ys_b[cn][y_i]])
                t = P.dma("sp", yout[qi // 4, cn, :, (qi % 4) * 512:(qi % 4 + 1) * 512], ys[:, cn, y_i],
                          reads=[ys_b[cn][y_i]])
                atoks.setdefault(qi // 4, []).append(t)
                if len(atoks[qi // 4]) == 8:
                    i = qi // 4
                    P.allgather(yout[i, 0:2].rearrange("c p t -> (c p) t"), yg[(i * 2) * 1024:(i * 2 + 1) * 1024, :],
                                atoks[i])

        for cn in range(2):
            stage1(cn, 0)
        for n in range(NBLK):
            if n >= 185:
                rec.pump(P, 2)
            for cn in range(2):
                pe_tri(cn, n)
            for cn in range(2):
                act_x(cn, n)
            if n + 1 < NBLK:
                for cn in range(2):
                    stage1(cn, n + 1)
            for cn in range(2):
                pe_comp(cn, n)
            for cn in range(2):
                dve_a(cn, n)
            if n >= 1:
                for cn in range(2):
                    pe_av(cn, n - 1)
        for cn in range(2):
            pe_av(cn, NBLK - 1)
        rec.pump(P, 1 << 30)
        P.barrier(cc=0)
        P.emit()
    return P.ncc


def _dram_in(nc, name, shape, dt=F32):
    return nc.dram_tensor(name, list(shape), dt, kind="ExternalInput").ap()


def _dram_out(nc, name, shape, dt=F32):
    return nc.dram_tensor(name, list(shape), dt, kind="ExternalOutput").ap()


def _dram_tmp(nc, name, shape, dt=F32):
    return nc.dram_tensor(name, list(shape), dt, kind="Internal").ap()


def build_fused():
    nc = bass.Bass("TRN2", target_bir_lowering=False)
    x = _dram_in(nc, "x", [TOK, D])
    ident = _dram_in(nc, "ident", [128, 128], BF16)
    gidx = _dram_in(nc, "gidx", [128, 80], mybir.dt.int32)
    invcnt = _dram_in(nc, "invcnt", [128, S])
    masks = _dram_in(nc, "masks", [128, 4, 512])
    tri = _dram_in(nc, "tri", [128, 2, 128], BF16)
    gfin = _dram_in(nc, "gfin", [D])
    out = _dram_out(nc, "out", [TOK, D])
    W = []
    for l in range(DEPTH):
        w = {}
        for tag in ("f1", "f2"):
            w[tag] = (_dram_in(nc, "%s_g%d" % (tag, l), [D]), _dram_in(nc, "%s_gate%d" % (tag, l), [D, DFF]),
                      _dram_in(nc, "%s_up%d" % (tag, l), [D, DFF]), _dram_in(nc, "%s_down%d" % (tag, l), [DFF, D]))
        w["gmix"] = _dram_in(nc, "gmix%d" % l, [D])
        w["win"] = _dram_in(nc, "win%d" % l, [D, DIN])
        w["wout"] = _dram_in(nc, "wout%d" % l, [D, D])
        w["pvec"] = _dram_in(nc, "pvec%d" % l, [128, 16])
        w["mats"] = _dram_in(nc, "mats%d" % l, [3, 128, 128])
        W.append(w)
    xa = _dram_tmp(nc, "xa", [TOK, D])
    xb = _dram_tmp(nc, "xb", [TOK, D])
    xc = _dram_tmp(nc, "xc", [TOK, D])
    zqk = _dram_tmp(nc, "zqk", [4, 4, 128, TOK], BF16)
    zv = _dram_tmp(nc, "zv", [4, 128, 16, 256], BF16)
    zr = _dram_tmp(nc, "zr", [4, 3, 128, TOK], F32)
    ys = _dram_tmp(nc, "ys", [4, 4, 128, TOK], BF16)
    zqk_g = _dram_tmp(nc, "zqk_g", [4 * 4 * 4 * 128, TOK], BF16)
    zv_g = _dram_tmp(nc, "zv_g", [4 * 4 * 128, 16 * 256], BF16)
    zr_g = _dram_tmp(nc, "zr_g", [4 * 4 * 3 * 128, TOK], F32)
    yg = _dram_tmp(nc, "yg", [4 * 4 * 4 * 128, TOK], BF16)
    with ExitStack() as es:
        P = Prog(nc, es)
        cur = x
        for l in range(DEPTH):
            w = W[l]
            ffn_phase(P, cur, xa, w["f1"][0], w["f1"][1], w["f1"][2], w["f1"][3], ident)
            cc_all = win_phase(P, xa, w["gmix"], w["win"], ident, zqk, zv, zr, zqk_g, zv_g, zr_g)
            cc_y = attn_phase(P, zqk_g, zv_g, zr_g, gidx, masks, tri, w["pvec"], w["mats"], invcnt, ys, yg, cc_all)
            wout_phase(P, xa, xb, yg, gidx, w["wout"], cc_y)
            ffn_phase(P, xb, xc, w["f2"][0], w["f2"][1], w["f2"][2], w["f2"][3], ident)
            cur = xc
        fnorm_phase(P, cur, out, gfin)
    return nc


_CACHE = {}


def _consts():
    ident = np.eye(128, dtype=np.float32).astype(ml_dtypes.bfloat16)
    s = np.arange(128)[:, None]
    t = np.arange(512)[None, :]
    masks = np.stack([(t > r * 128 + s).astype(np.float32) for r in range(4)], axis=1)
    jj = np.arange(128)[:, None]
    ss = np.arange(128)[None, :]
    tri = (jj >= ss).astype(np.float32)
    comp = 1.0 - tri
    tri2 = np.stack([tri, comp], axis=1).astype(ml_dtypes.bfloat16)
    pos = np.arange(S, dtype=np.float32) + 1.0
    invcnt = [np.ascontiguousarray(np.broadcast_to((1.0 / np.minimum(pos, float(w)))[None, :], (128, S)))
              .astype(np.float32) for w in (2, 4, 8, 16)]
    return ident, masks, tri2, invcnt


def _gidx(r):
    g = np.zeros((128, 80), np.int32)
    p = np.arange(128)
    for i in range(4):
        for c in range(4):
            g[:, i * 4 + c] = (r * 2 + c // 2) * 1024 + i * 256 + (c % 2) * 128 + p
        for c in range(3):
            g[:, 16 + i * 3 + c] = (r * 3 + c) * 512 + i * 128 + p
        g[:, 28 + i] = r * 512 + i * 128 + p
        for c in range(3):
            for h in range(2):
                g[:, 48 + (i * 3 + c) * 2 + h] = 2 * ((r * 3 + c) * 512 + i * 128 + p) + h
    for j in range(4):
        for c in range(4):
            g[:, 32 + j * 4 + c] = (r * 2 + c // 2) * 1024 + j * 256 + (c % 2) * 128 + p
    return g


def _mix_params(inp, l, j):
    pv = np.zeros((128, 16), np.float32)
    sl = slice(j * 128, (j + 1) * 128)
    pv[:, 0:4] = inp["conv_w"][l][:, sl].T
    pv[:, 4] = inp["conv_b"][l][sl]
    pv[:, 5] = inp["rg_b_a"][l][sl]
    pv[:, 6] = inp["rg_b_x"][l][sl]
    pv[:, 7] = inp["rg_lambda"][l][sl]
    pv[:, 8] = inp["pool_scale"][l][sl]
    pv[:, 9 + j] = 1.0
    mats = np.stack([inp["rg_w_a"][l][j], inp["rg_w_x"][l][j], inp["pool_w"][l][j]], axis=0)
    return pv, np.ascontiguousarray(mats, dtype=np.float32)


def kernel(**inp):
    inp = {k: np.asarray(v) for k, v in inp.items()}
    ident, masks, tri2, invcnt = _consts()
    xs = inp["x"].reshape(NCORE, TOK, D)
    if "nc" not in _CACHE:
        _CACHE["nc"] = build_fused()
    nc = _CACHE["nc"]
    base = dict(ident=ident, masks=masks, tri=tri2, gfin=inp["norm_final"])
    for l in range(DEPTH):
        for tag, which in (("f1", "ffn1"), ("f2", "ffn2")):
            base["%s_g%d" % (tag, l)] = inp["norm_" + which][l]
            base["%s_gate%d" % (tag, l)] = inp[which + "_gate"][l]
            base["%s_up%d" % (tag, l)] = inp[which + "_up"][l]
            base["%s_down%d" % (tag, l)] = inp[which + "_down"][l]
        base["gmix%d" % l] = inp["norm_mix"][l]
        base["win%d" % l] = inp["w_in"][l]
        base["wout%d" % l] = inp["w_out"][l]
    maps = []
    for c in range(NCORE):
        r = c % 4
        m = dict(base, x=xs[c], gidx=_gidx(r), invcnt=invcnt[r])
        for l in range(DEPTH):
            pv, mats = _mix_params(inp, l, r)
            m["pvec%d" % l] = pv
            m["mats%d" % l] = mats
        maps.append(m)
    res = run_bass_kernel_spmd(nc, maps, core_ids=list(range(NCORE)))
    out = np.stack([r["out"] for r in res.results], axis=0)
    return out.reshape(NB, S, D).astype(np.float32)
```
